# Optimizing a Trainium2 kernel written in Bass

```python
import functools
import jax, jax.numpy as jnp
from jax import lax
import numpy as np

D_MODEL = 1024
BATCH = 32
SEQ = 256
DEPTH = 2
DEC_BATCH = 4
DEC_SEQ = 4096
PAST_LEN = 512

GRID_W = 64
WIN_ROWS = 8
WIN_COLS = 16
N_HEADS = 8
HEAD_DIM = 64
ATTN_DIM = N_HEADS * HEAD_DIM
POOL_GROUPS = 4
POOL_GROUP_DIM = 64
POOL_DIM = POOL_GROUPS * POOL_GROUP_DIM
POOL_WINDOWS = (2, 4, 8, 16)
CONV_DIM = 256
CONV_WIDTH = 3
N_BRANCH = 3
FFN_DIM = 2816
N_MOD = 9
Q_BLOCK = 128
EPS = 1e-6
IN_OFFSETS = (ATTN_DIM, 2 * ATTN_DIM, 3 * ATTN_DIM,
              3 * ATTN_DIM + POOL_DIM,
              3 * ATTN_DIM + POOL_DIM + CONV_DIM,
              3 * ATTN_DIM + POOL_DIM + 2 * CONV_DIM,
              3 * ATTN_DIM + POOL_DIM + 3 * CONV_DIM)
IN_COLS = IN_OFFSETS[-1] + N_BRANCH * D_MODEL

kernel_name = "hybrid_diffusion_na_pool_conv_step"


def rmsnorm(x, g):
    x32 = x.astype(jnp.float32)
    y = x32 * lax.rsqrt(jnp.mean(x32 * x32, axis=-1, keepdims=True) + EPS)
    return y.astype(x.dtype) * g


def modulate(x, shift, scale):
    return x * (1 + scale) + shift


def adaln(cvec, w_mod, b_mod):
    m = jax.nn.silu(cvec) @ w_mod + b_mod
    return m.reshape(cvec.shape[0], N_MOD, D_MODEL)


def swiglu(h, w_gate, w_up, w_down):
    return (jax.nn.silu(h @ w_gate) * (h @ w_up)) @ w_down


def to_heads(t):
    B, L, _ = t.shape
    return t.reshape(B, L, N_HEADS, HEAD_DIM).transpose(0, 2, 1, 3)


def softmax_f32(s, dtype):
    return jax.nn.softmax(s.astype(jnp.float32), axis=-1).astype(dtype)


def context_attention(q, k, v):
    B, H, L, hd = q.shape
    nb = L // Q_BLOCK
    scale = HEAD_DIM ** -0.5
    qb = q.reshape(B, H, nb, Q_BLOCK, hd).transpose(2, 0, 1, 3, 4)

    def block(qi):
        s = jnp.einsum('bhqd,bhkd->bhqk', qi, k).astype(jnp.float32) * scale
        return jnp.einsum('bhqk,bhkd->bhqd', softmax_f32(s, v.dtype), v)

    o = lax.map(block, qb)
    return o.transpose(1, 2, 0, 3, 4).reshape(B, H, L, hd)


def neighbourhood_attention(q, k, v, k_ctx, v_ctx, rpb):
    B, H, T, hd = q.shape
    rows = T // GRID_W
    win_r = min(WIN_ROWS, rows)
    n_loc = win_r * WIN_COLS
    scale = HEAD_DIM ** -0.5
    kg = k.reshape(B, H, rows, GRID_W, hd)
    vg = v.reshape(B, H, rows, GRID_W, hd)
    q_rows = q.reshape(B, H, rows, GRID_W, hd).transpose(2, 0, 1, 3, 4)
    cols = jnp.arange(GRID_W)
    col_start = jnp.clip(cols - WIN_COLS // 2, 0, GRID_W - WIN_COLS)
    col_idx = col_start[:, None] + jnp.arange(WIN_COLS)[None, :]
    col_off = col_idx - cols[:, None] + (WIN_COLS - 1)
    rpb_cols = rpb[:, :, col_off]

    def row_block(args):
        r, q_r = args
        rs = jnp.clip(r - win_r // 2, 0, rows - win_r)
        kb = lax.dynamic_slice_in_dim(kg, rs, win_r, axis=2)[:, :, :, col_idx]
        vb = lax.dynamic_slice_in_dim(vg, rs, win_r, axis=2)[:, :, :, col_idx]
        row_off = rs + jnp.arange(win_r) - r + (WIN_ROWS - 1)
        bias = rpb_cols[:, row_off].transpose(0, 2, 1, 3).astype(jnp.float32)
        s_loc = jnp.einsum('bhqd,bhrqjd->bhqrj', q_r, kb).astype(jnp.float32) * scale + bias[None]
        s_ctx = jnp.einsum('bhqd,bhkd->bhqk', q_r, k_ctx).astype(jnp.float32) * scale
        s = jnp.concatenate([s_loc.reshape(B, H, GRID_W, n_loc), s_ctx], axis=-1)
        p = softmax_f32(s, v.dtype)
        p_loc = p[..., :n_loc].reshape(B, H, GRID_W, win_r, WIN_COLS)
        return (jnp.einsum('bhqrj,bhrqjd->bhqd', p_loc, vb)
                + jnp.einsum('bhqk,bhkd->bhqd', p[..., n_loc:], v_ctx))

    o = lax.map(row_block, (jnp.arange(rows), q_rows))
    return o.transpose(1, 2, 0, 3, 4).reshape(B, H, T, hd)


def multiscale_pool(u, w_pool, pool_scale):
    B, L, _ = u.shape
    t = jnp.arange(L)
    ug = u.reshape(B, L, POOL_GROUPS, POOL_GROUP_DIM)
    outs = []
    for g, w in enumerate(POOL_WINDOWS):
        x_g = ug[:, :, g].astype(jnp.float32)
        cs = jnp.concatenate([jnp.zeros((B, 1, POOL_GROUP_DIM), jnp.float32),
                              jnp.cumsum(x_g, axis=1)], axis=1)
        lo = jnp.clip(t - w // 2, 0, L)
        hi = jnp.clip(t - w // 2 + w, 0, L)
        mean = (jnp.take(cs, hi, axis=1) - jnp.take(cs, lo, axis=1)) / (hi - lo).astype(jnp.float32)[None, :, None]
        outs.append((mean - x_g).astype(u.dtype))
    d = jnp.stack(outs, axis=2)
    y = jnp.einsum('blgc,gcd->blgd', d, w_pool).reshape(B, L, POOL_DIM)
    return y * pool_scale


def short_conv(z, w_conv, b_conv):
    y = lax.conv_general_dilated(z, w_conv[:, None, :], window_strides=(1,),
                                 padding=((CONV_WIDTH // 2, CONV_WIDTH // 2),),
                                 dimension_numbers=('NWC', 'WIO', 'NWC'),
                                 feature_group_count=CONV_DIM)
    return y + b_conv


def trunk_layer(x, mod, lp, attend):
    m = [mod[:, i, None, :] for i in range(N_MOD)]
    B, L, _ = x.shape
    h = modulate(rmsnorm(x, lp['g_ffn1']), m[0], m[1])
    x = x + 0.5 * m[2] * swiglu(h, lp['w_ffn1_gate'], lp['w_ffn1_up'], lp['w_ffn1_down'])
    n = modulate(rmsnorm(x, lp['g_mix']), m[3], m[4])
    proj = n @ lp['w_in']
    q, k, v, u_pool, u_conv, gate_b, gate_c, merge = jnp.split(proj, IN_OFFSETS, axis=-1)
    q = rmsnorm(to_heads(q), lp['g_q'])
    k = rmsnorm(to_heads(k), lp['g_k'])
    v = to_heads(v)
    a = attend(q, k, v).transpose(0, 2, 1, 3).reshape(B, L, ATTN_DIM)
    pl = multiscale_pool(u_pool, lp['w_pool'], lp['pool_scale'])
    cv = gate_b * short_conv(gate_c * u_conv, lp['w_conv'], lp['b_conv'])
    g_a, g_p, g_c = jnp.split(jax.nn.sigmoid(merge), N_BRANCH, axis=-1)
    merged = (g_a * (a @ lp['w_br_attn']) + g_p * (pl @ lp['w_br_pool'])
              + g_c * (cv @ lp['w_br_conv']))
    x = x + m[5] * (merged @ lp['w_out'])
    h = modulate(rmsnorm(x, lp['g_ffn2']), m[6], m[7])
    x = x + 0.5 * m[8] * swiglu(h, lp['w_ffn2_gate'], lp['w_ffn2_up'], lp['w_ffn2_down'])
    return x, k, v


def setup_inputs(seed: int = 0) -> dict:
    key = jax.random.key(seed)
    ks = iter(jax.random.split(key, 40))
    f32 = jnp.float32

    def nrm(shape, s=1.0):
        return jax.random.normal(next(ks), shape, f32) * s

    def gain(shape):
        return 1.0 + nrm(shape, 0.05)

    D = D_MODEL
    return {
        'x_prompt': nrm((BATCH, SEQ, D)),
        'x_sample': nrm((DEC_BATCH, DEC_SEQ, D)),
        'cache_k': nrm((DEC_BATCH, DEPTH, N_HEADS, PAST_LEN, HEAD_DIM)),
        'cache_v': nrm((DEC_BATCH, DEPTH, N_HEADS, PAST_LEN, HEAD_DIM)),
        'c': nrm((DEC_BATCH, D)),
        'c_ctx': nrm((D,)),
        'w_mod': nrm((DEPTH, D, N_MOD * D), 0.5 * D ** -0.5),
        'b_mod': nrm((DEPTH, N_MOD * D), 0.02),
        'g_ffn1': gain((DEPTH, D)),
        'w_ffn1_gate': nrm((DEPTH, D, FFN_DIM), D ** -0.5),
        'w_ffn1_up': nrm((DEPTH, D, FFN_DIM), D ** -0.5),
        'w_ffn1_down': nrm((DEPTH, FFN_DIM, D), FFN_DIM ** -0.5),
        'g_mix': gain((DEPTH, D)),
        'w_in': nrm((DEPTH, D, IN_COLS), D ** -0.5),
        'g_q': gain((DEPTH, HEAD_DIM)),
        'g_k': gain((DEPTH, HEAD_DIM)),
        'rpb': nrm((DEPTH, N_HEADS, 2 * WIN_ROWS - 1, 2 * WIN_COLS - 1), 0.5),
        'w_pool': nrm((DEPTH, POOL_GROUPS, POOL_GROUP_DIM, POOL_GROUP_DIM), POOL_GROUP_DIM ** -0.5),
        'pool_scale': 1.0 + nrm((DEPTH, POOL_DIM), 0.1),
        'w_conv': nrm((DEPTH, CONV_WIDTH, CONV_DIM), CONV_WIDTH ** -0.5),
        'b_conv': nrm((DEPTH, CONV_DIM), 0.02),
        'w_br_attn': nrm((DEPTH, ATTN_DIM, D), ATTN_DIM ** -0.5),
        'w_br_pool': nrm((DEPTH, POOL_DIM, D), POOL_DIM ** -0.5),
        'w_br_conv': nrm((DEPTH, CONV_DIM, D), CONV_DIM ** -0.5),
        'w_out': nrm((DEPTH, D, D), D ** -0.5),
        'g_ffn2': gain((DEPTH, D)),
        'w_ffn2_gate': nrm((DEPTH, D, FFN_DIM), D ** -0.5),
        'w_ffn2_up': nrm((DEPTH, D, FFN_DIM), D ** -0.5),
        'w_ffn2_down': nrm((DEPTH, FFN_DIM, D), FFN_DIM ** -0.5),
    }


def reference(x_prompt, x_sample, cache_k, cache_v, c, c_ctx, w_mod, b_mod,
              g_ffn1, w_ffn1_gate, w_ffn1_up, w_ffn1_down, g_mix, w_in, g_q, g_k, rpb,
              w_pool, pool_scale, w_conv, b_conv, w_br_attn, w_br_pool, w_br_conv, w_out,
              g_ffn2, w_ffn2_gate, w_ffn2_up, w_ffn2_down):
    xp = x_prompt
    xs = x_sample
    new_k = []
    new_v = []
    for l in range(DEPTH):
        lp = {
            'g_ffn1': g_ffn1[l], 'w_ffn1_gate': w_ffn1_gate[l], 'w_ffn1_up': w_ffn1_up[l],
            'w_ffn1_down': w_ffn1_down[l], 'g_mix': g_mix[l], 'w_in': w_in[l],
            'g_q': g_q[l], 'g_k': g_k[l], 'w_pool': w_pool[l], 'pool_scale': pool_scale[l],
            'w_conv': w_conv[l], 'b_conv': b_conv[l], 'w_br_attn': w_br_attn[l],
            'w_br_pool': w_br_pool[l], 'w_br_conv': w_br_conv[l], 'w_out': w_out[l],
            'g_ffn2': g_ffn2[l], 'w_ffn2_gate': w_ffn2_gate[l], 'w_ffn2_up': w_ffn2_up[l],
            'w_ffn2_down': w_ffn2_down[l],
        }
        mod_ctx = adaln(c_ctx[None, :], w_mod[l], b_mod[l])
        xp, kp, vp = trunk_layer(xp, mod_ctx, lp, context_attention)
        new_k.append(kp)
        new_v.append(vp)
        mod_lat = adaln(c, w_mod[l], b_mod[l])
        attend_lat = functools.partial(neighbourhood_attention, k_ctx=cache_k[:, l],
                                       v_ctx=cache_v[:, l], rpb=rpb[l])
        xs, _, _ = trunk_layer(xs, mod_lat, lp, attend_lat)
    return (xp, xs, jnp.stack(new_k, axis=1), jnp.stack(new_v, axis=1))
```

```python
import numpy as np
from contextlib import ExitStack
import concourse.bass as bass
import concourse.mybir as mybir
from concourse.bass_utils import run_bass_kernel_spmd

F32 = mybir.dt.float32
BF16 = mybir.dt.bfloat16
AF = mybir.ActivationFunctionType
ALU = mybir.AluOpType
AX = mybir.AxisListType

ENGS = ("pe", "act", "dve", "pool", "sp")
SB_BASE = 16512
SB_LIMIT = 229344
EPS = 1e-6
NEG = -30000.0
POOL_WINDOWS = (2, 4, 8, 16)


class Slot:
    __slots__ = ("name", "lw", "rd", "conf", "excl")

    def __init__(self, name, excl=False):
        self.name = name
        self.lw = None
        self.rd = {}
        self.conf = []
        self.excl = excl


class Prog:
    def __init__(self, nc):
        self.nc = nc
        self.ops = {e: [] for e in ENGS}
        self.seen = {e: {} for e in ENGS}
        self.dma_cnt = {}
        self.dry = False
        self.phase = ""
        self.sb_off = SB_BASE
        self.sb_slots = []

    def sb(self, name, shape, dtype, at=None, nslots=1):
        esz = 4 if dtype == F32 else 2
        nbytes = int(np.prod(shape[1:])) * esz
        if at is None:
            off = (self.sb_off + 31) // 32 * 32
            self.sb_off = off + nbytes
            assert self.sb_off <= SB_LIMIT, (name, self.sb_off)
        else:
            off = at
            assert off % 32 == 0 and off + nbytes <= SB_LIMIT, (name, off, nbytes)
        h = self.nc.alloc_sbuf_tensor_at(name, list(shape), dtype, offset=off)
        slots = [Slot(f"{name}.{i}") for i in range(nslots)]
        for (lo, hi, sl) in self.sb_slots:
            if lo < off + nbytes and off < hi:
                for a in sl:
                    for b in slots:
                        a.conf.append(b)
                        b.conf.append(a)
        self.sb_slots.append((off, off + nbytes, slots))
        return h, slots

    def _record(self, eng, fn, reads, writes, dma_key=None):
        if self.dry:
            return None
        deps = {}

        def add(tok):
            if tok is None:
                return
            p, v = tok
            if deps.get(p, -1) < v:
                deps[p] = v

        for s in reads:
            add(s.lw)
            if s.excl:
                for t in s.rd.values():
                    if t[0] != eng:
                        add(t)
            for c in s.conf:
                add(c.lw)
        for s in writes:
            add(s.lw)
            for t in s.rd.values():
                add(t)
            for c in s.conf:
                add(c.lw)
                for t in c.rd.values():
                    add(t)
        waits = []
        seen = self.seen[eng]
        for p, v in deps.items():
            if p == "pe" and eng == "pe":
                continue
            if seen.get(p, -1) >= v:
                continue
            seen[p] = v
            waits.append((p, v))
            if not p.startswith("#"):
                self.ops[p][v]["signal"] = True
        idx = len(self.ops[eng])
        rec = {"fn": fn, "waits": waits, "signal": False, "dma": dma_key, "tag": self.phase}
        self.ops[eng].append(rec)
        if dma_key is not None:
            cnt = self.dma_cnt.get(dma_key, 0) + 16
            self.dma_cnt[dma_key] = cnt
            tok = ("#" + dma_key, cnt)
        else:
            tok = (eng, idx)
        for s in reads:
            s.rd[tok[0]] = tok
        for s in writes:
            s.lw = tok
            s.rd = {}
        return tok

    def op(self, eng, fn, reads=(), writes=()):
        return self._record(eng, fn, list(reads), list(writes))

    def dma(self, queue, fn, reads, writes, key):
        return self._record(queue, fn, list(reads), list(writes), dma_key=key)

    def emit(self):
        nc = self.nc
        with ExitStack() as es:
            sem = {e: es.enter_context(nc.semaphore("s_" + e)) for e in ENGS}
            dsem = {"#" + k: es.enter_context(nc.semaphore("d_" + k)) for k in self.dma_cnt}
            for e in ENGS:
                n = 0
                for rec in self.ops[e]:
                    if rec["dma"] is None and rec["signal"]:
                        n += 1
                        rec["seq"] = n
            ops = self.ops

            def run(e, eng):
                for rec in ops[e]:
                    for (p, v) in rec["waits"]:
                        if p.startswith("#"):
                            eng.wait_ge(dsem[p], v)
                        else:
                            eng.wait_ge(sem[p], ops[p][v]["seq"])
                    inst = rec["fn"](eng)
                    if rec["dma"] is not None:
                        inst.then_inc(dsem["#" + rec["dma"]], 16)
                    elif rec["signal"]:
                        inst.then_inc(sem[e], 1)

            with nc.Block() as block:
                @block.tensor
                def _(eng):
                    run("pe", eng)

                @block.scalar
                def _(eng):
                    run("act", eng)

                @block.vector
                def _(eng):
                    run("dve", eng)

                @block.gpsimd
                def _(eng):
                    run("pool", eng)

                @block.sync
                def _(eng):
                    run("sp", eng)


class Stream:
    NCONV = 16

    def __init__(self, P, name, queue, bufs, slots, cache=None):
        self.P, self.name, self.queue = P, name, queue
        self.bufs, self.slots = bufs, slots
        self.plan = []
        self.i = 0
        self.issued = 0
        self.cache = cache
        self.cmap = {}
        self.nconv = 0
        self.kslots = {}

    def reset(self):
        self.i = 0
        self.issued = 0
        self.cmap = {}

    def convert(self, pred):
        mk = lambda d, s: (lambda e: e.dma_start(out=d, in_=s))
        for (parts, barrier, cid, ncols) in self.plan:
            if cid is None or self.cache is None or cid in self.cmap or not pred(cid):
                continue
            idx = len(self.cmap)
            sl = Slot(f"{self.name}c{idx}")
            self.cmap[cid] = (idx, sl)
            key = f"{self.name}v{self.nconv % self.NCONV}"
            self.nconv += 1
            ks = self.kslots.setdefault(key, Slot(key))
            for (dstf, src) in parts:
                dst = dstf(self.cache[idx])
                self.P.dma("pool", mk(dst, src), [], [sl, ks], key)

    def _issue(self, k):
        nb = len(self.bufs)
        b = k % nb
        parts, barrier, cid, ncols = self.plan[k]
        mk = lambda d, s: (lambda e: e.dma_start(out=d, in_=s))
        if cid is None or self.cache is None:
            for (dstf, src) in parts:
                dst = dstf(self.bufs[b])
                self.P.dma(self.queue, mk(dst, src), [], [self.slots[b]], f"{self.name}{b}")
        else:
            idx, sl = self.cmap[cid]
            self.P.dma("sp", mk(self.bufs[b][:, 0:ncols], self.cache[idx, :, 0:ncols]), [sl], [self.slots[b]], f"{self.name}h{b}")

    def get(self, parts, barrier=False, cid=None, ncols=None, hold=0):
        k = self.i
        self.i += 1
        nb = len(self.bufs)
        if self.P.dry:
            self.plan.append((parts, barrier, cid, ncols))
            return self.bufs[k % nb], self.slots[k % nb]
        while self.issued < min(len(self.plan), k + nb - hold):
            if self.issued > k and self.plan[self.issued][1]:
                break
            self._issue(self.issued)
            self.issued += 1
        return self.bufs[k % nb], self.slots[k % nb]


def build_program():
    nc = bass.Bass("TRN2", target_bir_lowering=False)
    P = Prog(nc)

    def din(name, shape):
        return nc.dram_tensor(name, list(shape), F32, kind="ExternalInput").ap()

    def dout(name, shape):
        return nc.dram_tensor(name, list(shape), F32, kind="ExternalOutput").ap()

    xp_d = din("xp", [128, 8, 1024])
    xs_d = din("xs", [128, 8, 2560])
    cc_d = din("cc", [128, 16])
    wmod_d = din("w_mod", [2, 1024, 9216])
    bmod_d = din("bmod", [128, 2 * 72])
    gv_d = din("gv", [128, 2 * 3 * 8])
    wg_d = [din("w_g1", [2, 1024, 2816]), din("w_g2", [2, 1024, 2816])]
    wu_d = [din("w_u1", [2, 1024, 2816]), din("w_u2", [2, 1024, 2816])]
    wd_d = [din("w_d1", [2, 2816, 1024]), din("w_d2", [2, 2816, 1024])]
    win_d = din("w_in", [2, 1024, 5632])
    gqk_d = din("gqk", [128, 4])
    gkb_d = din("gkb", [128, 2 * 64])
    btab_d = din("btab", [2, 3, 8, 128, 6 * 256])
    wpbd_d = din("wpbd", [128, 2 * 2 * 128])
    pcv_d = din("pcv", [128, 2 * 5 * 2])
    wba_d = din("w_ba", [2, 512, 1024])
    wbp_d = din("w_bp", [2, 256, 1024])
    wbc_d = din("w_bc", [2, 256, 1024])
    wout_d = din("w_out", [2, 1024, 1024])
    ckT_d = din("ckT", [2, 4, 128, 512])
    cvv_d = din("cvv", [2, 4, 128, 512])
    invc_d = din("invc", [4, 128, 2 * 512])
    yp_d = dout("yp", [128, 8, 1024])
    ys_d = dout("ys", [128, 8, 2560])
    nk_d = dout("nk", [1024, 2, 512])
    nv_d = dout("nv", [1024, 2, 512])
    out_slots = []
    dbg_d = dout("dbg", [128, 8, 512]) if DEBUG_LIMIT is not None else None

    X, sX = P.sb("X", [128, 8, 2560], F32, nslots=40)
    KT, (sKT,) = P.sb("KT", [128, 4, 2560], BF16)
    VA, (sVA,) = P.sb("VA", [128, 20, 512], BF16)
    UP, (sUP,) = P.sb("UP", [128, 2, 2576], BF16)
    ZZ, (sZZ,) = P.sb("ZZ", [128, 2, 2576], BF16)
    WBb, WBs = [], []
    for i in range(3):
        h, (s,) = P.sb(f"WB{i}", [128, 4096], BF16)
        WBb.append(h)
        WBs.append(s)
    ONESB, (sONES,) = P.sb("ONESB", [128, 128], BF16)
    BD64, (sBD64,) = P.sb("BD64", [128, 128], BF16)
    ONES64, (sONES64,) = P.sb("ONES64", [128, 64], BF16)
    DER, (sDER,) = P.sb("DER", [128, 2, 2, 9, 8], F32)
    GQK, (sGQK,) = P.sb("GQK", [128, 2, 2], F32)
    GKB, (sGKB,) = P.sb("GKB", [128, 2, 64], F32)
    PCV, (sPCV,) = P.sb("PCV", [128, 2, 5, 2], F32)
    WPB, (sWPB,) = P.sb("WPB", [128, 2, 2, 128], BF16)
    SS8, (sSS8,) = P.sb("SS8", [128, 8], F32)
    SCR, sSCR = [], []
    for i in range(2):
        h, (s,) = P.sb(f"SCR{i}", [128, 544], F32)
        SCR.append(h)
        sSCR.append(s)
    RSTD, (sRSTD,) = P.sb("RSTD", [128, 512], F32)
    SQ2, (sSQ2,) = P.sb("SQ2", [128, 512], BF16)
    SS64 = [RSTD[:, 0:256], RSTD[:, 256:512]]
    sSS64 = [sRSTD, sRSTD]
    U0 = (P.sb_off + 31) // 32 * 32
    NT, sNTc = P.sb("NT", [128, 8, 512], BF16, at=U0, nslots=8)
    M0 = U0 + 8192
    K = 1024
    CC, (sCC,) = P.sb("CC", [128, 16], F32, at=M0)
    SCB, (sSCB,) = P.sb("SCB", [128, 8, 2], BF16, at=M0 + 64)
    BMOD, (sBMOD,) = P.sb("BMOD", [128, 2, 72], F32, at=M0 + 128)
    GV, (sGV,) = P.sb("GV", [128, 2, 3, 8], F32, at=M0 + 128 + 576)
    MODS, (sMODS,) = P.sb("MODS", [128, 2, 2, 72], F32, at=M0 + 1024)
    ACTT, (sACTT,) = P.sb("ACTT", [128, 22, 512], BF16, at=M0)
    SQ, sSQc = P.sb("SQ", [128, 8, 512], BF16, at=M0, nslots=8)
    NT2, sNT2c = P.sb("NT2", [128, 8, 512], BF16, at=M0, nslots=8)
    SQ8, sSQ8c = P.sb("SQ8", [128, 8, 512], BF16, at=M0 + 16 * K, nslots=8)
    OUTK, (sOUTK,) = P.sb("OUTK", [128, 512], F32, at=M0 + 8 * K)
    OUTV, (sOUTV,) = P.sb("OUTV", [128, 512], F32, at=M0 + 10 * K)
    UCV, (sUCV,) = P.sb("UCV", [128, 2, 512], F32, at=M0 + 12 * K)
    QT, (sQT,) = P.sb("QT", [128, 4, 512], BF16, at=M0)
    AT, (sAT,) = P.sb("AT", [128, 4, 512], BF16, at=M0 + 4 * K)
    PT, sPT = [], []
    for i in range(2):
        h, (s,) = P.sb(f"PT{i}", [128, 512], BF16, at=M0 + 8 * K + i * K)
        PT.append(h)
        sPT.append(s)
    BTb, BTs_ = [], []
    for i in range(2):
        h, (s,) = P.sb(f"BT{i}", [128, 1536], BF16, at=M0 + 10 * K + i * 3 * K)
        BTb.append(h)
        BTs_.append(s)
    CKT, (sCKT,) = P.sb("CKT", [128, 4, 512], BF16, at=M0 + 16 * K)
    CVA, (sCVA,) = P.sb("CVA", [128, 4, 512], BF16, at=M0 + 20 * K)
    PL, (sPL,) = P.sb("PL", [128, 2, 512], BF16, at=M0)
    CVT, (sCVT,) = P.sb("CVT", [128, 2, 512], BF16, at=M0 + 2 * K)
    TA, (sTA,) = P.sb("TA", [128, 2, 544], F32, at=M0 + 8 * K)
    SUMT, (sSUMT,) = P.sb("SUMT", [128, 2, 512], F32, at=M0 + 8 * K + 4352)
    INVC, (sINVC,) = P.sb("INVC", [128, 2, 512], F32, at=M0 + 8 * K + 4352 + 4096)
    DD, (sDD,) = P.sb("DD", [128, 2, 512], BF16, at=M0 + 8 * K + 4352 + 8192)
    MG, sMG = P.sb("MG", [128, 8, 512], F32, at=M0 + 8 * K, nslots=8)
    assert M0 + 24 * K <= SB_LIMIT, (M0 + 24 * K, SB_LIMIT)

    PB, sPB = [], []
    for i in range(8):
        PB.append(nc.alloc_psum_tensor(f"PB{i}", [128, 512], F32))
        sPB.append(Slot(f"PB{i}", excl=True))

    wbc_h = nc.dram_tensor("wb_cache", [116, 128, 4096], BF16).ap()
    WB = Stream(P, "wb", "pool", WBb, WBs, cache=wbc_h)
    WS = WB
    btc_h = nc.dram_tensor("bt_cache", [48, 128, 1536], BF16).ap()
    BT = Stream(P, "bt", "pool", BTb, BTs_, cache=btc_h)

    def mm(out, lhsT, rhs, start, stop, reads, writes):
        P.op("pe", lambda e: e.matmul(out, lhsT=lhsT, rhs=rhs, start=start, stop=stop), reads, writes)

    def act(out, in_, func, reads, writes, bias=None, scale=None):
        kw = {}
        if bias is not None:
            kw["bias"] = bias
        if scale is not None:
            kw["scale"] = scale
        P.op("act", lambda e: e.activation(out=out, in_=in_, func=func, **kw), reads, writes)

    def tt(out, in0, in1, op, reads, writes, eng="dve"):
        P.op(eng, lambda e: e.tensor_tensor(out=out, in0=in0, in1=in1, op=op), reads, writes)

    def ts(out, in0, s1, s2, op0, op1, reads, writes, eng="dve"):
        if op1 is None:
            P.op(eng, lambda e: e.tensor_scalar(out=out, in0=in0, scalar1=s1, scalar2=None, op0=op0), reads, writes)
        else:
            P.op(eng, lambda e: e.tensor_scalar(out=out, in0=in0, scalar1=s1, scalar2=s2, op0=op0, op1=op1), reads, writes)

    def stt(out, in0, scalar, in1, op0, op1, reads, writes, eng="dve"):
        P.op(eng, lambda e: e.scalar_tensor_tensor(out=out, in0=in0, scalar=scalar, in1=in1, op0=op0, op1=op1), reads, writes)

    def recip(out, in_, reads, writes):
        P.op("dve", lambda e: e.reciprocal(out=out, in_=in_), reads, writes)

    def copy(out, in_, reads, writes, eng="dve"):
        if eng == "act":
            P.op("act", lambda e: e.copy(out=out, in_=in_), reads, writes)
        else:
            P.op(eng, lambda e: e.tensor_copy(out=out, in_=in_), reads, writes)

    def memset(ap, val, writes, eng="dve"):
        P.op(eng, lambda e: e.memset(ap, val), [], writes)

    def dma(queue, out, in_, reads, writes, key):
        P.dma(queue, lambda e: e.dma_start(out=out, in_=in_), reads, writes, key)

    def wview(buf, kc, n):
        return buf[:, 0:kc * n].rearrange("p (k c) -> p k c", c=n)

    def wsrc(d_ap2):
        return d_ap2.rearrange("(kc p) c -> p kc c", p=128)

    psc = {"n": 0}

    def rot(lst):
        psc["n"] += 1
        return lst[psc["n"] % len(lst)]

    stg = {"n": 0}
    cur = {}

    def stage():
        stg["n"] += 1
        if DEBUG_LIMIT is not None and stg["n"] > DEBUG_LIMIT:
            raise StopBody()

    def body():
        try:
            body_inner()
        except StopBody:
            for t in range(cur["ntile"]):
                dma("sp", cur["y_d"][:, :, t * 512:(t + 1) * 512], X[:, :, t * 512:(t + 1) * 512], [sX[t * 8 + c] for c in range(8)], [new_out()], f"x{t}")
            P.op("sp", lambda e: e.nop(), list(out_slots), [])

    def body_inner():
        stg["n"] = 0
        psc["n"] = 0
        P.phase = "prologue"
        del out_slots[:]
        if not P.dry:
            WB.convert(lambda cid: cid[0] in ("gu", "d") and cid[1] == 0 and cid[2] == 0)
        memset(ONESB[:, :], 1.0 / 1024.0, [sONES])
        memset(BD64[:, :], 0.0, [sBD64])
        memset(BD64[0:64, 0:64], 1.0 / 64.0, [sBD64])
        memset(BD64[64:128, 64:128], 1.0 / 64.0, [sBD64])
        memset(ONES64[:, :], 1.0, [sONES64])
        dma("sp", CC[:, :], cc_d, [], [sCC], "c0")
        dma("sp", BMOD[:, :, :].rearrange("p a b -> p (a b)"), bmod_d, [], [sBMOD], "c1")
        dma("sp", GV[:, :, :, :].rearrange("p a b c -> p (a b c)"), gv_d, [], [sGV], "c2")
        dma("sp", GQK[:, :, :].rearrange("p a b -> p (a b)"), gqk_d, [], [sGQK], "c3")
        dma("sp", GKB[:, :, :].rearrange("p a b -> p (a b)"), gkb_d, [], [sGKB], "c4")
        dma("sp", PCV[:, :, :, :].rearrange("p a b c -> p (a b c)"), pcv_d, [], [sPCV], "c5")
        dma("pool", WPB[:, :, :, :].rearrange("p a b c -> p (a b c)"), wpbd_d, [], [sWPB], "c6")
        act(SCB[:, :, :].rearrange("p a b -> p (a b)"), CC[:, :], AF.Silu, [sCC], [sSCB])

        for l in range(2):
            pm = PB[7]
            for piece in range(18):
                buf, bs = WB.get([(lambda b: wview(b, 8, 512), wsrc(wmod_d[l, :, piece * 512:(piece + 1) * 512]))])
                bv = wview(buf, 8, 512)
                for fc in range(4):
                    f = piece * 4 + fc
                    for kc in range(8):
                        mm(pm[:, 2 * f:2 * f + 2], bv[:, kc, fc * 128:(fc + 1) * 128], SCB[:, kc, :],
                           kc == 0, kc == 7, [bs, sSCB], [sPB[7]])
            pmv = pm[:, 0:144].rearrange("p (f s) -> p f s", s=2)
            for s in range(2):
                tt(MODS[:, l, s, :], pmv[:, :, s], BMOD[:, l, :], ALU.add, [sPB[7], sBMOD], [sMODS])
            for s in range(2):
                def mv(i):
                    return MODS[:, l, s, i * 8:(i + 1) * 8]
                for (di, gi, sc_i) in ((0, 0, 1), (3, 1, 4), (6, 2, 7)):
                    stt(DER[:, l, s, di, :], mv(sc_i), 1.0, GV[:, l, gi, :], ALU.add, ALU.mult, [sMODS, sGV], [sDER])
                for (di, sh_i) in ((1, 0), (4, 3), (7, 6)):
                    copy(DER[:, l, s, di, :], mv(sh_i), [sMODS], [sDER])
                ts(DER[:, l, s, 2, :], mv(2), 0.5, None, ALU.mult, None, [sMODS], [sDER])
                copy(DER[:, l, s, 5, :], mv(5), [sMODS], [sDER])
                ts(DER[:, l, s, 8, :], mv(8), 0.5, None, ALU.mult, None, [sMODS], [sDER])

        if not P.dry:
            for lyr in range(2):
                WB.convert(lambda cid: cid[0] in ("gu", "d") and cid[1] == 0 and cid[2] == lyr)
                WB.convert(lambda cid: cid[0] in ("k", "v", "pc", "q") and cid[1] == lyr)
                WB.convert(lambda cid: cid[0] in ("gc", "gb") and cid[1] == lyr)
                WB.convert(lambda cid: cid[0] in ("mg", "o") and cid[1] == lyr)
                WB.convert(lambda cid: cid[0] in ("gu", "d") and cid[1] == 1 and cid[2] == lyr)
            BT.convert(lambda cid: True)

        def der(l, s, i, c):
            return DER[:, l, s, i, c:c + 1]

        def xs_(t, c):
            return sX[t * 8 + c]

        norm_done = {"key": None}
        NTb = [(NT, sNTc), (NT2, sNT2c)]

        def norm_mod(l, s, t, ia):
            if norm_done["key"] == (l, s, t, ia):
                norm_done["key"] = None
                return
            for c in range(8):
                norm_sq((l, s, t, ia), c, sq8=True)
            for c in range(8):
                norm_sqmm((l, s, t, ia), c, sq8=True)
            norm_fin()
            for c in range(8):
                norm_chunk((l, s, t, ia), c)

        def sqbuf(c):
            k = c % 3
            if k == 2:
                return SQ2[:, :], sSQ2
            return SCR[k][:, 0:256].bitcast(BF16), sSCR[k]

        def _sq_eng(s):
            if s == 1:
                return ["act", "dve", "pool", "act", "dve", "act", "dve", "pool"]
            return ["act", "dve", "act", "dve", "act", "dve", "act", "dve"]

        def norm_sq(key, c, sq8=False):
            l, s, t, ia = key
            pp, P.phase = P.phase, "norm"
            xin = X[:, c, t * 512:(t + 1) * 512]
            if sq8:
                sqv, ssl = SQ8[:, c, :], sSQ8c[c]
            else:
                sqv, ssl = sqbuf(c)
            e = _sq_eng(s)[c]
            if e == "act":
                act(sqv, xin, AF.Square, [xs_(t, c)], [ssl])
            else:
                tt(sqv, xin, xin, ALU.mult, [xs_(t, c)], [ssl], eng=e)
            P.phase = pp

        def norm_sqmm(key, c, sq8=False):
            pp, P.phase = P.phase, "norm"
            if sq8:
                sqv, ssl = SQ8[:, c, :], sSQ8c[c]
            else:
                sqv, ssl = sqbuf(c)
            mm(PB[0][:, :], ONESB[:, :], sqv, c == 0, c == 7, [ssl, sONES], [sPB[0]])
            P.phase = pp

        def norm_fin():
            pp, P.phase = P.phase, "norm"
            act(RSTD[:, :], PB[0][:, :], AF.Ln, [sPB[0]], [sRSTD], bias=EPS, scale=1.0)
            act(RSTD[:, :], RSTD[:, :], AF.Exp, [sRSTD], [sRSTD], scale=-0.5)
            P.phase = pp

        def norm_chunk(key, c, nb=0):
            l, s, t, ia = key
            pp, P.phase = P.phase, "norm"
            NTx, sNTx = NTb[nb]
            c0 = t * 512
            k = c % 2
            stt(SCR[k][:, 0:512], X[:, c, c0:c0 + 512], der(l, s, ia, c), RSTD[:, :], ALU.mult, ALU.mult,
                [xs_(t, c), sDER, sRSTD], [sSCR[k]])
            act(NTx[:, c, :], SCR[k][:, 0:512], AF.Identity, [sSCR[k], sDER], [sNTx[c]], bias=der(l, s, ia + 1, c), scale=1.0)
            P.phase = pp

        def ffn(l, s, t, which, nxt=None):
            ia = 0 if which == 0 else 6
            c0 = t * 512
            norm_mod(l, s, t, ia)
            P.phase = "ffn.gu"
            for j2 in range(11):
                P.phase = f"ffn.gu.{j2}"
                buf, bs = WB.get([
                    (lambda b: wview(b, 8, 512)[:, :, 0:256], wsrc(wg_d[which][l, :, j2 * 256:(j2 + 1) * 256])),
                    (lambda b: wview(b, 8, 512)[:, :, 256:512], wsrc(wu_d[which][l, :, j2 * 256:(j2 + 1) * 256])),
                ], cid=("gu", which, l, j2), ncols=4096)
                bv = wview(buf, 8, 512)
                for jj in range(2):
                    j = j2 * 2 + jj
                    ig = 1 + (j % 2)
                    iu = 3 + (j % 2)
                    for kc in range(8):
                        mm(PB[ig][:, :], bv[:, kc, jj * 128:(jj + 1) * 128], NT[:, kc, :], kc == 0, kc == 7, [bs, sNTc[kc]], [sPB[ig]])
                    for kc in range(8):
                        mm(PB[iu][:, :], bv[:, kc, 256 + jj * 128:256 + (jj + 1) * 128], NT[:, kc, :], kc == 0, kc == 7, [bs, sNTc[kc]], [sPB[iu]])
                    k = j % 2
                    act(SCR[k][:, 0:512], PB[ig][:, :], AF.Silu, [sPB[ig]], [sSCR[k]])
                    tt(ACTT[:, j, :], SCR[k][:, 0:512], PB[iu][:, :], ALU.mult, [sSCR[k], sPB[iu]], [sACTT])
            for dc in range(8):
                P.phase = f"ffn.down.{dc}"
                if nxt is not None and dc < 3:
                    for c in ((0, 1, 2), (3, 4, 5), (6, 7))[dc]:
                        norm_sq(nxt, c)
                buf, bs = WB.get([(lambda b: wview(b, 22, 128), wsrc(wd_d[which][l, :, dc * 128:(dc + 1) * 128]))], cid=("d", which, l, dc), ncols=2816)
                bv = wview(buf, 22, 128)
                ip = 5 + (dc % 2)
                for kc in range(22):
                    mm(PB[ip][:, :], bv[:, kc, :], ACTT[:, kc, :], kc == 0, kc == 21, [bs, sACTT], [sPB[ip]])
                stt(X[:, dc, c0:c0 + 512], PB[ip][:, :], der(l, s, ia + 2, dc), X[:, dc, c0:c0 + 512], ALU.mult, ALU.add,
                    [sPB[ip], sDER, xs_(t, dc)], [xs_(t, dc)])
                if nxt is not None:
                    if dc < 3:
                        for c in ((0, 1, 2), (3, 4, 5), (6, 7))[dc]:
                            norm_sqmm(nxt, c)
                    if dc == 2:
                        norm_fin()
                    if 3 <= dc <= 6:
                        norm_chunk(nxt, 2 * (dc - 3))
                        norm_chunk(nxt, 2 * (dc - 3) + 1)
            if nxt is not None:
                norm_done["key"] = nxt

        qkc = {"n": 0}

        def qknorm(ps, sps, dst, sdst, gcol):
            k = qkc["n"] % 2
            qkc["n"] += 1
            if k == 0:
                sqv, ssq = SQ2[:, :], sSQ2
                pb, spb = PB[0], sPB[0]
                rs, srs = RSTD[:, :], sRSTD
            else:
                sqv, ssq = SCR[1][:, 0:256].bitcast(BF16), sSCR[1]
                pb, spb = PB[7], sPB[7]
                rs, srs = SCR[0][:, 0:512], sSCR[0]
            act(sqv, ps, AF.Square, [sps], [ssq])
            mm(pb[:, :], BD64[:, :], sqv, True, True, [ssq, sBD64], [spb])
            act(rs, pb[:, :], AF.Ln, [spb], [srs], bias=EPS, scale=1.0)
            act(rs, rs, AF.Exp, [srs], [srs], scale=-0.5)
            stt(dst, ps, gcol, rs, ALU.mult, ALU.mult, [sps, sGQK, srs], [sdst])

        def win(l, c0, n):
            return wsrc(win_d[l, :, c0:c0 + n])

        def upv(buf, grp, t, c, lo, hi):
            if grp == "p":
                v = buf[:, c, 0:4 * 272].rearrange("p (s m) -> p s m", m=272)
                return v[:, 2 * t:2 * t + 2, 8 + lo:8 + hi]
            return buf[:, c, 8 + t * 512 + lo:8 + t * 512 + hi].unsqueeze(1)

        def seg(ap2, grp, n=None):
            if grp == "p":
                return ap2.rearrange("p (s m) -> p s m", m=256)
            return ap2.unsqueeze(1)

        def mix1(l, s, grp, t):
            c0 = t * 512
            norm_mod(l, s, t, 3)
            P.phase = "mix1"
            NT, sNTc = NTb[t % 2]
            ntile_g = 2 if grp == "p" else 5
            nkey = (l, s, t + 1, 3) if t + 1 < ntile_g else None
            if nkey is not None:
                for c in range(8):
                    norm_sq(nkey, c, sq8=True)
            buf, bs = WB.get([(lambda b: wview(b, 8, 512), win(l, 512, 512))], cid=("k", l), ncols=4096)
            bv = wview(buf, 8, 512)
            for pr in range(4):
                ip = 1 + pr
                for kc in range(8):
                    mm(PB[ip][:, :], bv[:, kc, pr * 128:(pr + 1) * 128], NT[:, kc, :], kc == 0, kc == 7, [bs, sNTc[kc]], [sPB[ip]])
            for pr in range(4):
                ip = 1 + pr
                qknorm(PB[ip][:, :], sPB[ip], KT[:, pr, c0:c0 + 512], sKT, GQK[:, l, 1:2])
            if nkey is not None:
                for c in range(8):
                    norm_sqmm(nkey, c, sq8=True)
                norm_fin()
                for c in range(8):
                    norm_chunk(nkey, c, nb=(t + 1) % 2)
                norm_done["key"] = nkey
            if grp == "p":
                P.phase = "mix1.ktm"
                for tb in range(4):
                    ip = 5 + (tb % 2)
                    for kc in range(8):
                        mm(PB[ip][:, :], NT[:, kc, tb * 128:(tb + 1) * 128], bv[:, kc, :], kc == 0, kc == 7, [bs, sNTc[kc]], [sPB[ip]])
                    act(SCR[0][:, 0:512], PB[ip][:, :], AF.Square, [sPB[ip]], [sSCR[0]])
                    P.op("dve", lambda e: e.tensor_reduce(out=SS8[:, :], in_=SCR[0][:, 0:512].rearrange("p (h d) -> p h d", d=64),
                                                         axis=AX.X, op=ALU.add), [sSCR[0]], [sSS8])
                    act(SS8[:, :], SS8[:, :], AF.Ln, [sSS8], [sSS8], bias=EPS, scale=1.0 / 64.0)
                    act(SS8[:, :], SS8[:, :], AF.Exp, [sSS8], [sSS8], scale=-0.5)
                    tt(OUTK[:, :].rearrange("p (h d) -> p h d", d=64), PB[ip][:, :].rearrange("p (h d) -> p h d", d=64),
                       SS8[:, :].unsqueeze(2).broadcast_to([128, 8, 64]), ALU.mult, [sPB[ip], sSS8], [sOUTK])
                    tt(OUTK[:, :].rearrange("p (h d) -> p h d", d=64), OUTK[:, :].rearrange("p (h d) -> p h d", d=64),
                       GKB[:, l, :].unsqueeze(1).broadcast_to([128, 8, 64]), ALU.mult, [sOUTK, sGKB], [sOUTK])
                    r0 = t * 512 + tb * 128
                    dma("sp", nk_d[r0:r0 + 128, l, :], OUTK[:, :], [sOUTK], [new_out()], "ok")
            P.phase = "mix1.v"
            buf, bs = WB.get([(lambda b: wview(b, 8, 512), win(l, 1024, 512))], cid=("v", l), ncols=4096)
            bv = wview(buf, 8, 512)
            for tb in range(4):
                ip = 3 + (tb % 2)
                for kc in range(8):
                    mm(PB[ip][:, :], NT[:, kc, tb * 128:(tb + 1) * 128], bv[:, kc, :], kc == 0, kc == 7, [bs, sNTc[kc]], [sPB[ip]])
                copy(VA[:, t * 4 + tb, :], PB[ip][:, :], [sPB[ip]], [sVA], eng="act")
                if grp == "p":
                    copy(OUTV[:, :], PB[ip][:, :], [sPB[ip]], [sOUTV])
                    r0 = t * 512 + tb * 128
                    dma("sp", nv_d[r0:r0 + 128, l, :], OUTV[:, :], [sOUTV], [new_out()], "ov")
            P.phase = "mix1.pc"
            buf, bs = WB.get([(lambda b: wview(b, 8, 512), win(l, 1536, 512))], cid=("pc", l), ncols=4096)
            bv = wview(buf, 8, 512)
            for c4 in range(4):
                ip = 1 + (c4 % 2)
                for kc in range(8):
                    mm(PB[ip][:, :], bv[:, kc, c4 * 128:(c4 + 1) * 128], NT[:, kc, :], kc == 0, kc == 7, [bs, sNTc[kc]], [sPB[ip]])
                if c4 < 2:
                    copy(upv(UP, grp, t, c4, 0, 512 if grp == "s" else 256), seg(PB[ip][:, :], grp), [sPB[ip]], [sUP], eng="act")
                else:
                    copy(UCV[:, c4 - 2, :], PB[ip][:, :], [sPB[ip]], [sUCV], eng="act")
            P.phase = "mix1.gc"
            buf, bs = WS.get([(lambda b: wview(b, 8, 256), win(l, 2304, 256))], cid=("gc", l), ncols=2048)
            bv = wview(buf, 8, 256)
            for c2 in range(2):
                ip = 3 + (c2 % 2)
                for kc in range(8):
                    mm(PB[ip][:, :], bv[:, kc, c2 * 128:(c2 + 1) * 128], NT[:, kc, :], kc == 0, kc == 7, [bs, sNTc[kc]], [sPB[ip]])
                tt(upv(ZZ, grp, t, c2, 0, 512 if grp == "s" else 256), seg(UCV[:, c2, :], grp), seg(PB[ip][:, :], grp), ALU.mult,
                   [sUCV, sPB[ip]], [sZZ])

        def attn_prompt(l, t):
            c0 = t * 512
            n = 0
            for sq in range(2):
                for h in range(8):
                    pr, po = h // 2, (h % 2) * 64
                    isb = 1 + (n % 3)
                    k = n % 2
                    io, idn = 4 + k, 6 + k
                    n += 1
                    qap = QT[po:po + 64, pr, sq * 256:(sq + 1) * 256]
                    for kb in range(2):
                        kc0 = c0 + sq * 256 + kb * 128
                        mm(PB[isb][:, kb * 256:(kb + 1) * 256], KT[po:po + 64, pr, kc0:kc0 + 128], qap, True, True, [sKT, sQT], [sPB[isb]])
                    act(PT[k][:, :], PB[isb][:, :], AF.Exp, [sPB[isb]], [sPT[k]], scale=0.125)
                    for kb in range(2):
                        mm(PB[io][0:64, 0:256], VA[:, t * 4 + sq * 2 + kb, h * 64:(h + 1) * 64], PT[k][:, kb * 256:(kb + 1) * 256],
                           kb == 0, kb == 1, [sVA, sPT[k]], [sPB[io]])
                    for kb in range(2):
                        mm(PB[idn][0:64, 0:256], ONES64[:, :], PT[k][:, kb * 256:(kb + 1) * 256],
                           kb == 0, kb == 1, [sONES64, sPT[k]], [sPB[idn]])
                    act(RSTD[0:64, 0:256], PB[idn][0:64, 0:256], AF.Ln, [sPB[idn]], [sRSTD])
                    act(RSTD[0:64, 0:256], RSTD[0:64, 0:256], AF.Exp, [sRSTD], [sRSTD], scale=-1.0)
                    tt(AT[po:po + 64, pr, sq * 256:(sq + 1) * 256], PB[io][0:64, 0:256], RSTD[0:64, 0:256], ALU.mult,
                       [sPB[io], sRSTD], [sAT])

        def attn_sample(l, t):
            cnt = {"s": 0, "e": 0}
            pending = []
            for h in range(8):
                pr, po = h // 2, (h % 2) * 64
                sts = []
                prev = None
                for jb in range(2):
                    j = 2 * t + jb
                    typ = 0 if j == 0 else (2 if j == 9 else 1)
                    brow = min(max(4 * j - 4, 0), 28)
                    if prev is None or prev[0] != typ:
                        bt, bts = BT.get([(lambda b: b[:, :], btab_d[l, typ, h, :, :])], barrier=(jb == 0 and h == 0), cid=("bt", l, typ, h), ncols=1536)
                        prev = (typ, bt, bts)
                    sts.append({"jb": jb, "kch0": brow // 2, "bt": prev[1], "bts": prev[2], "io": 4 + jb, "idn": 6 + jb,
                                "qap": QT[po:po + 64, pr, jb * 256:(jb + 1) * 256], "isb": {}})

                def emit_s(st, cp):
                    isb = 1 + (cnt["s"] % 3)
                    cnt["s"] += 1
                    st["isb"][cp] = isb
                    for ii in range(2):
                        if cp < 3:
                            kc0 = (st["kch0"] + cp * 2 + ii) * 128
                            lhs = KT[po:po + 64, pr, kc0:kc0 + 128]
                            rd = [sKT, sQT]
                        else:
                            i = (cp - 3) * 2 + ii
                            lhs = CKT[po:po + 64, pr, i * 128:(i + 1) * 128]
                            rd = [sCKT, sQT]
                        mm(PB[isb][:, ii * 256:(ii + 1) * 256], lhs, st["qap"], True, True, rd, [sPB[isb]])

                def emit_pv(st, cp):
                    isb = st["isb"][cp]
                    k = cnt["e"] % 2
                    cnt["e"] += 1
                    io, idn = st["io"], st["idn"]
                    if cp < 3:
                        stt(SCR[k][:, 0:512], PB[isb][:, :], 0.125, st["bt"][:, cp * 512:(cp + 1) * 512], ALU.mult, ALU.add,
                            [sPB[isb], st["bts"]], [sSCR[k]])
                        act(PT[k][:, :], SCR[k][:, 0:512], AF.Exp, [sSCR[k]], [sPT[k]])
                    else:
                        act(PT[k][:, :], PB[isb][:, :], AF.Exp, [sPB[isb]], [sPT[k]], scale=0.125)
                    for ii in range(2):
                        if cp < 3:
                            lhs = VA[:, st["kch0"] + cp * 2 + ii, h * 64:(h + 1) * 64]
                            rd = [sVA, sPT[k]]
                        else:
                            i = (cp - 3) * 2 + ii
                            lhs = CVA[:, i, h * 64:(h + 1) * 64]
                            rd = [sCVA, sPT[k]]
                        first = (cp == 0 and ii == 0)
                        last = (cp == 4 and ii == 1)
                        mm(PB[io][0:64, 0:256], lhs, PT[k][:, ii * 256:(ii + 1) * 256], first, last, rd, [sPB[io]])
                        mm(PB[idn][0:64, 0:256], ONES64[:, :], PT[k][:, ii * 256:(ii + 1) * 256], first, last,
                           [sONES64, sPT[k]], [sPB[idn]])

                A, B = sts
                emit_s(A, 0)
                emit_s(B, 0)
                for cp in range(5):
                    if cp + 1 < 5:
                        emit_s(A, cp + 1)
                    emit_pv(A, cp)
                    if cp + 1 < 5:
                        emit_s(B, cp + 1)
                    emit_pv(B, cp)

                def mk_norm(st, po=po, pr=pr):
                    def f():
                        jb = st["jb"]
                        io, idn = st["io"], st["idn"]
                        act(SS64[jb][0:64, 0:256], PB[idn][0:64, 0:256], AF.Ln, [sPB[idn]], [sSS64[jb]])
                        act(SS64[jb][0:64, 0:256], SS64[jb][0:64, 0:256], AF.Exp, [sSS64[jb]], [sSS64[jb]], scale=-1.0)
                        tt(AT[po:po + 64, pr, jb * 256:(jb + 1) * 256], PB[io][0:64, 0:256], SS64[jb][0:64, 0:256], ALU.mult,
                           [sPB[io], sSS64[jb]], [sAT])
                    return f
                mk_norm(A)()
                mk_norm(B)()

        def pool_conv(l, grp, t):
            S, n = (2, 256) if grp == "p" else (1, 512)
            m = n + 16
            if grp == "p":
                ktab = 0
            else:
                ktab = 1 if t == 0 else (3 if t == 4 else 2)
            dma("sp", INVC[:, :, :].rearrange("p a b -> p (a b)"), invc_d[ktab], [], [sINVC], "iv")

            def U(c, lo, hi, p0=0, p1=128):
                return upv(UP, grp, t, c, lo, hi)[p0:p1]

            def tv(h2, lo, hi, p0=0, p1=128):
                return h2.rearrange("p (s m) -> p s m", m=m)[p0:p1, :, 8 + lo:8 + hi]

            def TAv(c, lo, hi, p0=0, p1=128):
                return tv(TA[:, c, 0:S * m], lo, hi, p0, p1)

            def TBv(lo, hi, p0=0, p1=128):
                return tv(SCR[0][:, 0:S * m], lo, hi, p0, p1)

            def TCv(lo, hi, p0=0, p1=128):
                return tv(SCR[1][:, 0:S * m], lo, hi, p0, p1)

            def SUMv(c, p0=0, p1=128):
                return SUMT[p0:p1, c, :].rearrange("p (s m) -> p s m", m=n)

            add = ALU.add
            tt(TAv(0, -7, n + 7, 64, 128), U(0, -8, n + 6, 64, 128), U(0, -7, n + 7, 64, 128), add, [sUP], [sTA])
            tt(TAv(1, -7, n + 7), U(1, -8, n + 6), U(1, -7, n + 7), add, [sUP], [sTA])
            tt(SUMv(0, 0, 64), U(0, -1, n - 1, 0, 64), U(0, 0, n, 0, 64), add, [sUP], [sSUMT])
            tt(SUMv(0, 64, 128), TAv(0, -1, n - 1, 64, 128), TAv(0, 1, n + 1, 64, 128), add, [sTA], [sSUMT])
            tt(TBv(-6, n + 6), TAv(1, -7, n + 5), TAv(1, -5, n + 7), add, [sTA], [sSCR[0]])
            tt(SUMv(1, 0, 64), TBv(-2, n - 2, 0, 64), TBv(2, n + 2, 0, 64), add, [sSCR[0]], [sSUMT])
            tt(TCv(-4, n + 4, 64, 128), TBv(-6, n + 2, 64, 128), TBv(-2, n + 6, 64, 128), add, [sSCR[0]], [sSCR[1]])
            tt(SUMv(1, 64, 128), TCv(-4, n - 4, 64, 128), TCv(4, n + 4, 64, 128), add, [sSCR[1]], [sSUMT])
            for c in range(2):
                tt(SUMT[:, c, :], SUMT[:, c, :], INVC[:, c, :], ALU.mult, [sSUMT, sINVC], [sSUMT])
                tt(seg(DD[:, c, :], grp), SUMv(c), U(c, 0, n), ALU.subtract, [sSUMT, sUP], [sDD])
            for c in range(2):
                ip = 1 + (c % 2)
                mm(PB[ip][:, :], WPB[:, l, c, :], DD[:, c, :], True, True, [sWPB, sDD], [sPB[ip]])
                act(PL[:, c, :], PB[ip][:, :], AF.Identity, [sPB[ip], sPCV], [sPL], scale=PCV[:, l, 0, c:c + 1])
            buf, bs = WS.get([(lambda b: wview(b, 8, 256), win(l, 2048, 256))], cid=("gb", l), ncols=2048)
            bv = wview(buf, 8, 256)

            def Zv(c, lo, hi):
                return upv(ZZ, grp, t, c, lo, hi)

            for c in range(2):
                acc = SUMv(c)
                ts(acc, Zv(c, -1, n - 1), PCV[:, l, 1, c:c + 1], PCV[:, l, 4, c:c + 1], ALU.mult, ALU.add, [sZZ, sPCV, sSUMT], [sSUMT])
                stt(acc, Zv(c, 0, n), PCV[:, l, 2, c:c + 1], acc, ALU.mult, ALU.add, [sZZ, sPCV, sSUMT], [sSUMT])
                stt(acc, Zv(c, 1, n + 1), PCV[:, l, 3, c:c + 1], acc, ALU.mult, ALU.add, [sZZ, sPCV, sSUMT], [sSUMT])
                ip = 3 + (c % 2)
                for kc in range(8):
                    mm(PB[ip][:, :], bv[:, kc, c * 128:(c + 1) * 128], NT[:, kc, :], kc == 0, kc == 7, [bs, sNTc[kc]], [sPB[ip]])
                tt(CVT[:, c, :], SUMT[:, c, :], PB[ip][:, :], ALU.mult, [sSUMT, sPB[ip]], [sCVT])

        def mix2(l, s, grp, t):
            c0 = t * 512
            norm_mod(l, s, t, 3)
            if grp == "s":
                dma("pool", CKT[:, :, :], ckT_d[l].rearrange("a p k -> p a k"), [], [sCKT], "ck")
                dma("pool", CVA[:, :, :], cvv_d[l].rearrange("a p k -> p a k"), [], [sCVA], "cv")
            P.phase = "mix2.q"
            buf, bs = WB.get([(lambda b: wview(b, 8, 512), win(l, 0, 512))], cid=("q", l), ncols=4096)
            bv = wview(buf, 8, 512)
            for pr in range(4):
                ip = 1 + pr
                for kc in range(8):
                    mm(PB[ip][:, :], bv[:, kc, pr * 128:(pr + 1) * 128], NT[:, kc, :], kc == 0, kc == 7, [bs, sNTc[kc]], [sPB[ip]])
            for pr in range(4):
                ip = 1 + pr
                qknorm(PB[ip][:, :], sPB[ip], QT[:, pr, :], sQT, GQK[:, l, 0:1])
            P.phase = "attn." + grp
            if grp == "p":
                attn_prompt(l, t)
            else:
                attn_sample(l, t)
            P.phase = "poolconv"
            pool_conv(l, grp, t)
            P.phase = "merge"
            if DEBUG_LIMIT is not None and grp == "p" and t == 0 and l == 0:
                dma("pool", dbg_d[:, 0:4, :], AT[:, :, :], [sAT], [new_out()], "dbg")
                dma("pool", dbg_d[:, 4:6, :], PL[:, :, :], [sPL], [new_out()], "dbg")
                dma("pool", dbg_d[:, 6:8, :], CVT[:, :, :], [sCVT], [new_out()], "dbg")
            for br, (gc0, wbr, kcn, SRC, ssrc) in enumerate(((2560, wba_d, 4, AT, sAT), (3584, wbp_d, 2, PL, sPL), (4608, wbc_d, 2, CVT, sCVT))):
                for qt in range(4):
                    P.phase = f"merge.{br}.{qt}"
                    nb_cols = kcn * 256
                    gbuf, gs = WB.get([
                        (lambda b: b[:, 0:2048].rearrange("p (k c) -> p k c", c=256), win(l, gc0 + qt * 256, 256)),
                        (lambda b, kcn=kcn: b[:, 2048:2048 + kcn * 256].rearrange("p (k c) -> p k c", c=256),
                         wsrc(wbr[l, :, qt * 256:(qt + 1) * 256])),
                    ], cid=("mg", l, br, qt), ncols=2048 + nb_cols)
                    gv = gbuf[:, 0:2048].rearrange("p (k c) -> p k c", c=256)
                    bbv = gbuf[:, 2048:2048 + nb_cols].rearrange("p (k c) -> p k c", c=256)
                    bbs = gs
                    for d2 in range(2):
                        dc = qt * 2 + d2
                        ig, ib = 1 + (dc % 2), 3 + (dc % 2)
                        for kc in range(8):
                            mm(PB[ig][:, :], gv[:, kc, d2 * 128:(d2 + 1) * 128], NT[:, kc, :], kc == 0, kc == 7, [gs, sNTc[kc]], [sPB[ig]])
                        for kc in range(kcn):
                            mm(PB[ib][:, :], bbv[:, kc, d2 * 128:(d2 + 1) * 128], SRC[:, kc, :], kc == 0, kc == kcn - 1, [bbs, ssrc], [sPB[ib]])
                        k = dc % 2
                        act(SCR[k][:, 0:512], PB[ig][:, :], AF.Sigmoid, [sPB[ig]], [sSCR[k]])
                        if br == 0:
                            tt(MG[:, dc, :], SCR[k][:, 0:512], PB[ib][:, :], ALU.mult, [sSCR[k], sPB[ib]], [sMG[dc]])
                        else:
                            tt(SCR[k][:, 0:512], SCR[k][:, 0:512], PB[ib][:, :], ALU.mult, [sSCR[k], sPB[ib]], [sSCR[k]])
                            tt(MG[:, dc, :], MG[:, dc, :], SCR[k][:, 0:512], ALU.add, [sMG[dc], sSCR[k]], [sMG[dc]])
            P.phase = "outproj"
            for dc in range(8):
                copy(NT[:, dc, :], MG[:, dc, :], [sMG[dc]], [sNTc[dc]], eng="act")
            for half in range(2):
                buf, bs = WB.get([(lambda b: wview(b, 8, 512), wsrc(wout_d[l, :, half * 512:(half + 1) * 512]))], cid=("o", l, half), ncols=4096)
                bv = wview(buf, 8, 512)
                for d4 in range(4):
                    dc = half * 4 + d4
                    ip = 5 + (d4 % 2)
                    for kc in range(8):
                        mm(PB[ip][:, :], bv[:, kc, d4 * 128:(d4 + 1) * 128], NT[:, kc, :], kc == 0, kc == 7, [bs, sNTc[kc]], [sPB[ip]])
                    stt(X[:, dc, c0:c0 + 512], PB[ip][:, :], der(l, s, 5, dc), X[:, dc, c0:c0 + 512], ALU.mult, ALU.add,
                        [sPB[ip], sDER, xs_(t, dc)], [xs_(t, dc)])

        for grp, ntile, x_d, y_d, s in (("p", 2, xp_d, yp_d, 0), ("s", 5, xs_d, ys_d, 1)):
            if grp not in DEBUG_GROUPS:
                continue
            cur["ntile"], cur["y_d"] = ntile, y_d
            for t in range(ntile):
                dma("sp", X[:, :, t * 512:(t + 1) * 512], x_d[:, :, t * 512:(t + 1) * 512], [], [sX[t * 8 + c] for c in range(8)], f"x{t}")
            memset(UP[:, :, :], 0.0, [sUP])
            memset(ZZ[:, :, :], 0.0, [sZZ])
            for l in range(2):
                stage()
                for t in range(ntile):
                    ffn(l, s, t, 0, nxt=((l, s, t + 1, 0) if t + 1 < ntile else (l, s, 0, 3)))
                stage()
                for t in range(ntile):
                    mix1(l, s, grp, t)
                stage()
                for t in range(ntile):
                    mix2(l, s, grp, t)
                stage()
                for t in range(ntile):
                    ffn(l, s, t, 1, nxt=((l, s, t + 1, 6) if t + 1 < ntile else ((l + 1, s, 0, 0) if l == 0 else None)))
            for t in range(ntile):
                dma("sp", y_d[:, :, t * 512:(t + 1) * 512], X[:, :, t * 512:(t + 1) * 512], [sX[t * 8 + c] for c in range(8)], [new_out()], f"x{t}")
        P.op("sp", lambda e: e.nop(), list(out_slots), [])

    def new_out():
        sl = Slot("o")
        out_slots.append(sl)
        return sl

    P.dry = True
    body()
    P.dry = False
    for st in (WB, BT):
        st.reset()
    body()
    P.emit()
    _CACHE["pe_tags"] = [r["tag"] for r in P.ops["pe"]]
    return nc


_CACHE = {}
DEBUG_LIMIT = None
DEBUG_GROUPS = "ps"


class StopBody(Exception):
    pass


def _host_tables(rpb):
    key = np.arange(768)
    krel = key // 64
    kcol = key % 64
    q = np.arange(256)
    qr = q // 64
    qc = q % 64
    cs = np.clip(qc - 8, 0, 48)
    out = np.empty((2, 3, 8, 768, 256), np.float32)
    for typ in range(3):
        qrr = {0: qr, 1: 4 + qr, 2: 8 + qr}[typ]
        ws = {0: 0 * qr, 1: qr, 2: 4 + 0 * qr}[typ]
        valid = ((krel[:, None] >= ws[None, :]) & (krel[:, None] < ws[None, :] + 8)
                 & (kcol[:, None] >= cs[None, :]) & (kcol[:, None] < cs[None, :] + 16))
        ro = np.clip(krel[:, None] - qrr[None, :] + 7, 0, 14)
        co = np.clip(kcol[:, None] - qc[None, :] + 15, 0, 30)
        vals = rpb[:, :, ro, co]
        out[:, typ] = np.where(valid[None, None], vals, np.float32(NEG))
    out = out.reshape(2, 3, 8, 6, 128, 256).transpose(0, 1, 2, 4, 3, 5)
    return np.ascontiguousarray(out).reshape(2, 3, 8, 128, 6 * 256)


def _invc_tables():
    tabs = np.empty((4, 128, 2, 512), np.float32)
    n = np.arange(512)
    for k in range(4):
        if k == 0:
            pos, Lq = n % 256, 256
        elif k == 1:
            pos, Lq = n, 2560
        elif k == 2:
            pos, Lq = n + 1024, 2560
        else:
            pos, Lq = n + 2048, 2560
        for c in range(2):
            for hf in range(2):
                w = POOL_WINDOWS[2 * c + hf]
                lo = np.clip(pos - w // 2, 0, Lq)
                hi = np.clip(pos - w // 2 + w, 0, Lq)
                tabs[k, hf * 64:(hf + 1) * 64, c, :] = (np.float32(1.0) / (hi - lo).astype(np.float32))[None, :]
    return tabs.reshape(4, 128, 1024)


def kernel(x_prompt, x_sample, cache_k, cache_v, c, c_ctx, w_mod, b_mod,
           g_ffn1, w_ffn1_gate, w_ffn1_up, w_ffn1_down, g_mix, w_in, g_q, g_k, rpb,
           w_pool, pool_scale, w_conv, b_conv, w_br_attn, w_br_pool, w_br_conv, w_out,
           g_ffn2, w_ffn2_gate, w_ffn2_up, w_ffn2_down):
    f = lambda a: np.ascontiguousarray(np.asarray(a, dtype=np.float32))
    x_prompt, x_sample, cache_k, cache_v = f(x_prompt), f(x_sample), f(cache_k), f(cache_v)
    c, c_ctx = f(c), f(c_ctx)
    if "nc" not in _CACHE:
        _CACHE["nc"] = build_program()
    nc = _CACHE["nc"]

    def fm(v):
        return v.reshape(8, 128).T

    bmod = f(b_mod).reshape(2, 72, 128).transpose(2, 0, 1).reshape(128, 144)
    gv = np.stack([np.stack([fm(f(g)[l]) for g in (g_ffn1, g_mix, g_ffn2)], axis=1) for l in range(2)], axis=1)
    p64 = np.arange(128) % 64
    gqk = np.stack([np.stack([f(g_q)[l][p64], f(g_k)[l][p64]], axis=1) for l in range(2)], axis=1)
    gkb = np.broadcast_to(f(g_k)[None], (128, 2, 64))
    pcv = np.empty((128, 2, 5, 2), np.float32)
    for l in range(2):
        pcv[:, l, 0, :] = f(pool_scale)[l].reshape(2, 128).T
        for k in range(3):
            pcv[:, l, 1 + k, :] = f(w_conv)[l, k].reshape(2, 128).T
        pcv[:, l, 4, :] = f(b_conv)[l].reshape(2, 128).T
    wpbd = np.zeros((128, 2, 2, 128), np.float32)
    wp = f(w_pool)
    for l in range(2):
        for cchunk in range(2):
            wpbd[0:64, l, cchunk, 0:64] = wp[l, 2 * cchunk]
            wpbd[64:128, l, cchunk, 64:128] = wp[l, 2 * cchunk + 1]
    btab = _host_tables(f(rpb))
    invc = _invc_tables()
    shared = {
        "w_mod": f(w_mod), "bmod": f(bmod), "gv": f(gv.reshape(128, 48)),
        "w_g1": f(w_ffn1_gate), "w_u1": f(w_ffn1_up), "w_d1": f(w_ffn1_down),
        "w_g2": f(w_ffn2_gate), "w_u2": f(w_ffn2_up), "w_d2": f(w_ffn2_down),
        "w_in": f(w_in), "gqk": f(gqk.reshape(128, 4)), "gkb": f(gkb.reshape(128, 128)),
        "btab": btab, "wpbd": f(wpbd.reshape(128, 512)), "pcv": f(pcv.reshape(128, 20)),
        "w_ba": f(w_br_attn), "w_bp": f(w_br_pool), "w_bc": f(w_br_conv), "w_out": f(w_out),
        "invc": invc,
    }
    in_maps = []
    for core in range(8):
        b, half = core // 2, core % 2
        r0 = 0 if half == 0 else 24
        xs = x_sample[b, r0 * 64:r0 * 64 + 2560].reshape(2560, 8, 128).transpose(2, 1, 0)
        xp = x_prompt[4 * core:4 * core + 4].reshape(1024, 8, 128).transpose(2, 1, 0)
        cc = np.stack([fm(c_ctx), fm(c[b])], axis=2).reshape(128, 16)
        ckT = cache_k[b].reshape(2, 4, 2, 512, 64).transpose(0, 1, 2, 4, 3).reshape(2, 4, 128, 512)
        cvv = cache_v[b].reshape(2, 8, 4, 128, 64).transpose(0, 2, 3, 1, 4).reshape(2, 4, 128, 512)
        m = dict(shared)
        m.update({"xs": f(xs), "xp": f(xp), "cc": f(cc), "ckT": f(ckT), "cvv": f(cvv)})
        in_maps.append(m)
    res = run_bass_kernel_spmd(nc, in_maps, core_ids=list(range(8)))
    y_prompt = np.empty((32, 256, 1024), np.float32)
    y_sample = np.empty((4, 4096, 1024), np.float32)
    new_k = np.empty((32, 2, 8, 256, 64), np.float32)
    new_v = np.empty((32, 2, 8, 256, 64), np.float32)
    for core in range(8):
        r = res.results[core]
        b, half = core // 2, core % 2
        yp = np.asarray(r["yp"]).transpose(2, 1, 0).reshape(4, 256, 1024)
        y_prompt[4 * core:4 * core + 4] = yp
        ys = np.asarray(r["ys"]).transpose(2, 1, 0).reshape(2560, 1024)
        if half == 0:
            y_sample[b, 0:2048] = ys[0:2048]
        else:
            y_sample[b, 2048:4096] = ys[512:2560]
        nk = np.asarray(r["nk"]).reshape(4, 256, 2, 8, 64).transpose(0, 2, 3, 1, 4)
        nv = np.asarray(r["nv"]).reshape(4, 256, 2, 8, 64).transpose(0, 2, 3, 1, 4)
        new_k[4 * core:4 * core + 4] = nk
        new_v[4 * core:4 * core + 4] = nv
    return (y_prompt, y_sample, new_k, new_v)
```

```python
import numpy as np
from contextlib import ExitStack
import concourse.bass as bass
import concourse.mybir as mybir
from concourse.bass_utils import run_bass_kernel_spmd

F32 = mybir.dt.float32
BF16 = mybir.dt.bfloat16
AF = mybir.ActivationFunctionType
ALU = mybir.AluOpType
AX = mybir.AxisListType

ENGS = ("pe", "act", "dve", "pool", "sp")
SB_BASE = 16512
SB_LIMIT = 229344
EPS = 1e-6
NEG = -30000.0
POOL_WINDOWS = (2, 4, 8, 16)


class Slot:
    __slots__ = ("name", "lw", "rd", "conf", "excl")

    def __init__(self, name, excl=False):
        self.name = name
        self.lw = None
        self.rd = {}
        self.conf = []
        self.excl = excl


class Prog:
    def __init__(self, nc):
        self.nc = nc
        self.ops = {e: [] for e in ENGS}
        self.seen = {e: {} for e in ENGS}
        self.dma_cnt = {}
        self.dry = False
        self.phase = ""
        self.sb_off = SB_BASE
        self.sb_slots = []

    def sb(self, name, shape, dtype, at=None, nslots=1):
        esz = 4 if dtype == F32 else 2
        nbytes = int(np.prod(shape[1:])) * esz
        if at is None:
            off = (self.sb_off + 31) // 32 * 32
            self.sb_off = off + nbytes
            assert self.sb_off <= SB_LIMIT, (name, self.sb_off)
        else:
            off = at
            assert off % 32 == 0 and off + nbytes <= SB_LIMIT, (name, off, nbytes)
        h = self.nc.alloc_sbuf_tensor_at(name, list(shape), dtype, offset=off)
        slots = [Slot(f"{name}.{i}") for i in range(nslots)]
        for (lo, hi, sl) in self.sb_slots:
            if lo < off + nbytes and off < hi:
                for a in sl:
                    for b in slots:
                        a.conf.append(b)
                        b.conf.append(a)
        self.sb_slots.append((off, off + nbytes, slots))
        return h, slots

    def _record(self, eng, fn, reads, writes, dma_key=None):
        if self.dry:
            return None
        deps = {}

        def add(tok):
            if tok is None:
                return
            p, v = tok
            if deps.get(p, -1) < v:
                deps[p] = v

        for s in reads:
            add(s.lw)
            if s.excl:
                for t in s.rd.values():
                    if t[0] != eng:
                        add(t)
            for c in s.conf:
                add(c.lw)
        for s in writes:
            add(s.lw)
            for t in s.rd.values():
                add(t)
            for c in s.conf:
                add(c.lw)
                for t in c.rd.values():
                    add(t)
        waits = []
        seen = self.seen[eng]
        for p, v in deps.items():
            if p == "pe" and eng == "pe":
                continue
            if seen.get(p, -1) >= v:
                continue
            seen[p] = v
            waits.append((p, v))
            if not p.startswith("#"):
                self.ops[p][v]["signal"] = True
        idx = len(self.ops[eng])
        rec = {"fn": fn, "waits": waits, "signal": False, "dma": dma_key, "tag": self.phase}
        self.ops[eng].append(rec)
        if dma_key is not None:
            cnt = self.dma_cnt.get(dma_key, 0) + 16
            self.dma_cnt[dma_key] = cnt
            tok = ("#" + dma_key, cnt)
        else:
            tok = (eng, idx)
        for s in reads:
            s.rd[tok[0]] = tok
        for s in writes:
            s.lw = tok
            s.rd = {}
        return tok

    def op(self, eng, fn, reads=(), writes=()):
        return self._record(eng, fn, list(reads), list(writes))

    def dma(self, queue, fn, reads, writes, key):
        return self._record(queue, fn, list(reads), list(writes), dma_key=key)

    def emit(self):
        nc = self.nc
        with ExitStack() as es:
            sem = {e: es.enter_context(nc.semaphore("s_" + e)) for e in ENGS}
            dsem = {"#" + k: es.enter_context(nc.semaphore("d_" + k)) for k in self.dma_cnt}
            for e in ENGS:
                n = 0
                for rec in self.ops[e]:
                    if rec["dma"] is None and rec["signal"]:
                        n += 1
                        rec["seq"] = n
            ops = self.ops

            def run(e, eng):
                for rec in ops[e]:
                    for (p, v) in rec["waits"]:
                        if p.startswith("#"):
                            eng.wait_ge(dsem[p], v)
                        else:
                            eng.wait_ge(sem[p], ops[p][v]["seq"])
                    inst = rec["fn"](eng)
                    if rec["dma"] is not None:
                        inst.then_inc(dsem["#" + rec["dma"]], 16)
                    elif rec["signal"]:
                        inst.then_inc(sem[e], 1)

            with nc.Block() as block:
                @block.tensor
                def _(eng):
                    run("pe", eng)

                @block.scalar
                def _(eng):
                    run("act", eng)

                @block.vector
                def _(eng):
                    run("dve", eng)

                @block.gpsimd
                def _(eng):
                    run("pool", eng)

                @block.sync
                def _(eng):
                    run("sp", eng)


class Stream:
    NCONV = 16

    def __init__(self, P, name, queue, bufs, slots, cache=None):
        self.P, self.name, self.queue = P, name, queue
        self.bufs, self.slots = bufs, slots
        self.plan = []
        self.i = 0
        self.issued = 0
        self.cache = cache
        self.cmap = {}
        self.nconv = 0
        self.kslots = {}

    def reset(self):
        self.i = 0
        self.issued = 0
        self.cmap = {}

    def convert(self, pred):
        mk = lambda d, s: (lambda e: e.dma_start(out=d, in_=s))
        for (parts, barrier, cid, ncols) in self.plan:
            if cid is None or self.cache is None or cid in self.cmap or not pred(cid):
                continue
            idx = len(self.cmap)
            sl = Slot(f"{self.name}c{idx}")
            self.cmap[cid] = (idx, sl)
            key = f"{self.name}v{self.nconv % self.NCONV}"
            self.nconv += 1
            ks = self.kslots.setdefault(key, Slot(key))
            for (dstf, src) in parts:
                dst = dstf(self.cache[idx])
                self.P.dma("pool", mk(dst, src), [], [sl, ks], key)

    def _issue(self, k):
        nb = len(self.bufs)
        b = k % nb
        parts, barrier, cid, ncols = self.plan[k]
        mk = lambda d, s: (lambda e: e.dma_start(out=d, in_=s))
        if cid is None or self.cache is None:
            for (dstf, src) in parts:
                dst = dstf(self.bufs[b])
                self.P.dma(self.queue, mk(dst, src), [], [self.slots[b]], f"{self.name}{b}")
        else:
            idx, sl = self.cmap[cid]
            self.P.dma("sp", mk(self.bufs[b][:, 0:ncols], self.cache[idx, :, 0:ncols]), [sl], [self.slots[b]], f"{self.name}h{b}")

    def get(self, parts, barrier=False, cid=None, ncols=None, hold=0):
        k = self.i
        self.i += 1
        nb = len(self.bufs)
        if self.P.dry:
            self.plan.append((parts, barrier, cid, ncols))
            return self.bufs[k % nb], self.slots[k % nb]
        while self.issued < min(len(self.plan), k + nb - hold):
            if self.issued > k and self.plan[self.issued][1]:
                break
            self._issue(self.issued)
            self.issued += 1
        return self.bufs[k % nb], self.slots[k % nb]


def build_program():
    nc = bass.Bass("TRN2", target_bir_lowering=False)
    P = Prog(nc)

    def din(name, shape):
        return nc.dram_tensor(name, list(shape), F32, kind="ExternalInput").ap()

    def dout(name, shape):
        return nc.dram_tensor(name, list(shape), F32, kind="ExternalOutput").ap()

    xp_d = din("xp", [128, 8, 1024])
    xs_d = din("xs", [128, 8, 2560])
    cc_d = din("cc", [128, 16])
    wmod_d = din("w_mod", [2, 1024, 9216])
    bmod_d = din("bmod", [128, 2 * 72])
    gv_d = din("gv", [128, 2 * 3 * 8])
    wg_d = [din("w_g1", [2, 1024, 2816]), din("w_g2", [2, 1024, 2816])]
    wu_d = [din("w_u1", [2, 1024, 2816]), din("w_u2", [2, 1024, 2816])]
    wd_d = [din("w_d1", [2, 2816, 1024]), din("w_d2", [2, 2816, 1024])]
    win_d = din("w_in", [2, 1024, 5632])
    gqk_d = din("gqk", [128, 4])
    gkb_d = din("gkb", [128, 2 * 64])
    btab_d = din("btab", [2, 3, 8, 128, 6 * 256])
    wpbd_d = din("wpbd", [128, 2 * 2 * 128])
    pcv_d = din("pcv", [128, 2 * 5 * 2])
    wba_d = din("w_ba", [2, 512, 1024])
    wbp_d = din("w_bp", [2, 256, 1024])
    wbc_d = din("w_bc", [2, 256, 1024])
    wout_d = din("w_out", [2, 1024, 1024])
    ckT_d = din("ckT", [2, 4, 128, 512])
    cvv_d = din("cvv", [2, 4, 128, 512])
    invc_d = din("invc", [4, 128, 2 * 512])
    yp_d = dout("yp", [128, 8, 1024])
    ys_d = dout("ys", [128, 8, 2560])
    nk_d = dout("nk", [1024, 2, 512])
    nv_d = dout("nv", [1024, 2, 512])
    out_slots = []
    dbg_d = dout("dbg", [128, 8, 512]) if DEBUG_LIMIT is not None else None

    X, sX = P.sb("X", [128, 8, 2560], F32, nslots=40)
    KT, (sKT,) = P.sb("KT", [128, 4, 2560], BF16)
    VA, (sVA,) = P.sb("VA", [128, 20, 512], BF16)
    UP, (sUP,) = P.sb("UP", [128, 2, 2576], BF16)
    ZZ, (sZZ,) = P.sb("ZZ", [128, 2, 2576], BF16)
    WBb, WBs = [], []
    for i in range(3):
        h, (s,) = P.sb(f"WB{i}", [128, 4096], BF16)
        WBb.append(h)
        WBs.append(s)
    ONESB, (sONES,) = P.sb("ONESB", [128, 128], BF16)
    BD64, (sBD64,) = P.sb("BD64", [128, 128], BF16)
    ONES64, (sONES64,) = P.sb("ONES64", [128, 64], BF16)
    DER, (sDER,) = P.sb("DER", [128, 2, 2, 9, 8], F32)
    GQK, (sGQK,) = P.sb("GQK", [128, 2, 2], F32)
    GKB, (sGKB,) = P.sb("GKB", [128, 2, 64], F32)
    PCV, (sPCV,) = P.sb("PCV", [128, 2, 5, 2], F32)
    WPB, (sWPB,) = P.sb("WPB", [128, 2, 2, 128], BF16)
    SS8, (sSS8,) = P.sb("SS8", [128, 8], F32)
    SCR, sSCR = [], []
    for i in range(2):
        h, (s,) = P.sb(f"SCR{i}", [128, 544], F32)
        SCR.append(h)
        sSCR.append(s)
    RSTD, (sRSTD,) = P.sb("RSTD", [128, 512], F32)
    SQ2, (sSQ2,) = P.sb("SQ2", [128, 512], BF16)
    SS64 = [RSTD[:, 0:256], RSTD[:, 256:512]]
    sSS64 = [sRSTD, sRSTD]
    U0 = (P.sb_off + 31) // 32 * 32
    NT, sNTc = P.sb("NT", [128, 8, 512], BF16, at=U0, nslots=8)
    M0 = U0 + 8192
    K = 1024
    CC, (sCC,) = P.sb("CC", [128, 16], F32, at=M0)
    SCB, (sSCB,) = P.sb("SCB", [128, 8, 2], BF16, at=M0 + 64)
    BMOD, (sBMOD,) = P.sb("BMOD", [128, 2, 72], F32, at=M0 + 128)
    GV, (sGV,) = P.sb("GV", [128, 2, 3, 8], F32, at=M0 + 128 + 576)
    MODS, (sMODS,) = P.sb("MODS", [128, 2, 2, 72], F32, at=M0 + 1024)
    ACTT, (sACTT,) = P.sb("ACTT", [128, 22, 512], BF16, at=M0)
    SQ, sSQc = P.sb("SQ", [128, 8, 512], BF16, at=M0, nslots=8)
    NT2, sNT2c = P.sb("NT2", [128, 8, 512], BF16, at=M0, nslots=8)
    SQ8, sSQ8c = P.sb("SQ8", [128, 8, 512], BF16, at=M0 + 16 * K, nslots=8)
    OUTK, (sOUTK,) = P.sb("OUTK", [128, 512], F32, at=M0 + 8 * K)
    OUTV, (sOUTV,) = P.sb("OUTV", [128, 512], F32, at=M0 + 10 * K)
    UCV, (sUCV,) = P.sb("UCV", [128, 2, 512], F32, at=M0 + 12 * K)
    QT, (sQT,) = P.sb("QT", [128, 4, 512], BF16, at=M0)
    AT, (sAT,) = P.sb("AT", [128, 4, 512], BF16, at=M0 + 4 * K)
    PT, sPT = [], []
    for i in range(2):
        h, (s,) = P.sb(f"PT{i}", [128, 512], BF16, at=M0 + 8 * K + i * K)
        PT.append(h)
        sPT.append(s)
    BTb, BTs_ = [], []
    for i in range(2):
        h, (s,) = P.sb(f"BT{i}", [128, 1536], BF16, at=M0 + 10 * K + i * 3 * K)
        BTb.append(h)
        BTs_.append(s)
    CKT, (sCKT,) = P.sb("CKT", [128, 4, 512], BF16, at=M0 + 16 * K)
    CVA, (sCVA,) = P.sb("CVA", [128, 4, 512], BF16, at=M0 + 20 * K)
    PL, (sPL,) = P.sb("PL", [128, 2, 512], BF16, at=M0)
    CVT, (sCVT,) = P.sb("CVT", [128, 2, 512], BF16, at=M0 + 2 * K)
    TA, (sTA,) = P.sb("TA", [128, 2, 544], F32, at=M0 + 8 * K)
    SUMT, (sSUMT,) = P.sb("SUMT", [128, 2, 512], F32, at=M0 + 8 * K + 4352)
    INVC, (sINVC,) = P.sb("INVC", [128, 2, 512], F32, at=M0 + 8 * K + 4352 + 4096)
    DD, (sDD,) = P.sb("DD", [128, 2, 512], BF16, at=M0 + 8 * K + 4352 + 8192)
    MG, sMG = P.sb("MG", [128, 8, 512], F32, at=M0 + 8 * K, nslots=8)
    assert M0 + 24 * K <= SB_LIMIT, (M0 + 24 * K, SB_LIMIT)

    PB, sPB = [], []
    for i in range(8):
        PB.append(nc.alloc_psum_tensor(f"PB{i}", [128, 512], F32))
        sPB.append(Slot(f"PB{i}", excl=True))

    wbc_h = nc.dram_tensor("wb_cache", [116, 128, 4096], BF16).ap()
    WB = Stream(P, "wb", "pool", WBb, WBs, cache=wbc_h)
    WS = WB
    btc_h = nc.dram_tensor("bt_cache", [48, 128, 1536], BF16).ap()
    BT = Stream(P, "bt", "pool", BTb, BTs_, cache=btc_h)

    def mm(out, lhsT, rhs, start, stop, reads, writes):
        P.op("pe", lambda e: e.matmul(out, lhsT=lhsT, rhs=rhs, start=start, stop=stop), reads, writes)

    def act(out, in_, func, reads, writes, bias=None, scale=None):
        kw = {}
        if bias is not None:
            kw["bias"] = bias
        if scale is not None:
            kw["scale"] = scale
        P.op("act", lambda e: e.activation(out=out, in_=in_, func=func, **kw), reads, writes)

    def tt(out, in0, in1, op, reads, writes, eng="dve"):
        P.op(eng, lambda e: e.tensor_tensor(out=out, in0=in0, in1=in1, op=op), reads, writes)

    def ts(out, in0, s1, s2, op0, op1, reads, writes, eng="dve"):
        if op1 is None:
            P.op(eng, lambda e: e.tensor_scalar(out=out, in0=in0, scalar1=s1, scalar2=None, op0=op0), reads, writes)
        else:
            P.op(eng, lambda e: e.tensor_scalar(out=out, in0=in0, scalar1=s1, scalar2=s2, op0=op0, op1=op1), reads, writes)

    def stt(out, in0, scalar, in1, op0, op1, reads, writes, eng="dve"):
        P.op(eng, lambda e: e.scalar_tensor_tensor(out=out, in0=in0, scalar=scalar, in1=in1, op0=op0, op1=op1), reads, writes)

    def recip(out, in_, reads, writes):
        P.op("dve", lambda e: e.reciprocal(out=out, in_=in_), reads, writes)

    def copy(out, in_, reads, writes, eng="dve"):
        if eng == "act":
            P.op("act", lambda e: e.copy(out=out, in_=in_), reads, writes)
        else:
            P.op(eng, lambda e: e.tensor_copy(out=out, in_=in_), reads, writes)

    def memset(ap, val, writes, eng="dve"):
        P.op(eng, lambda e: e.memset(ap, val), [], writes)

    def dma(queue, out, in_, reads, writes, key):
        P.dma(queue, lambda e: e.dma_start(out=out, in_=in_), reads, writes, key)

    def wview(buf, kc, n):
        return buf[:, 0:kc * n].rearrange("p (k c) -> p k c", c=n)

    def wsrc(d_ap2):
        return d_ap2.rearrange("(kc p) c -> p kc c", p=128)

    psc = {"n": 0}

    def rot(lst):
        psc["n"] += 1
        return lst[psc["n"] % len(lst)]

    stg = {"n": 0}
    cur = {}

    def stage():
        stg["n"] += 1
        if DEBUG_LIMIT is not None and stg["n"] > DEBUG_LIMIT:
            raise StopBody()

    def body():
        try:
            body_inner()
        except StopBody:
            for t in range(cur["ntile"]):
                dma("sp", cur["y_d"][:, :, t * 512:(t + 1) * 512], X[:, :, t * 512:(t + 1) * 512], [sX[t * 8 + c] for c in range(8)], [new_out()], f"x{t}")
            P.op("sp", lambda e: e.nop(), list(out_slots), [])

    def body_inner():
        stg["n"] = 0
        psc["n"] = 0
        P.phase = "prologue"
        del out_slots[:]
        if not P.dry:
            WB.convert(lambda cid: cid[0] in ("gu", "d") and cid[1] == 0 and cid[2] == 0)
        memset(ONESB[:, :], 1.0 / 1024.0, [sONES])
        memset(BD64[:, :], 0.0, [sBD64])
        memset(BD64[0:64, 0:64], 1.0 / 64.0, [sBD64])
        memset(BD64[64:128, 64:128], 1.0 / 64.0, [sBD64])
        memset(ONES64[:, :], 1.0, [sONES64])
        dma("sp", CC[:, :], cc_d, [], [sCC], "c0")
        dma("sp", BMOD[:, :, :].rearrange("p a b -> p (a b)"), bmod_d, [], [sBMOD], "c1")
        dma("sp", GV[:, :, :, :].rearrange("p a b c -> p (a b c)"), gv_d, [], [sGV], "c2")
        dma("sp", GQK[:, :, :].rearrange("p a b -> p (a b)"), gqk_d, [], [sGQK], "c3")
        dma("sp", GKB[:, :, :].rearrange("p a b -> p (a b)"), gkb_d, [], [sGKB], "c4")
        dma("sp", PCV[:, :, :, :].rearrange("p a b c -> p (a b c)"), pcv_d, [], [sPCV], "c5")
        dma("pool", WPB[:, :, :, :].rearrange("p a b c -> p (a b c)"), wpbd_d, [], [sWPB], "c6")
        act(SCB[:, :, :].rearrange("p a b -> p (a b)"), CC[:, :], AF.Silu, [sCC], [sSCB])

        for l in range(2):
            pm = PB[7]
            for piece in range(18):
                buf, bs = WB.get([(lambda b: wview(b, 8, 512), wsrc(wmod_d[l, :, piece * 512:(piece + 1) * 512]))])
                bv = wview(buf, 8, 512)
                for fc in range(4):
                    f = piece * 4 + fc
                    for kc in range(8):
                        mm(pm[:, 2 * f:2 * f + 2], bv[:, kc, fc * 128:(fc + 1) * 128], SCB[:, kc, :],
                           kc == 0, kc == 7, [bs, sSCB], [sPB[7]])
            pmv = pm[:, 0:144].rearrange("p (f s) -> p f s", s=2)
            for s in range(2):
                tt(MODS[:, l, s, :], pmv[:, :, s], BMOD[:, l, :], ALU.add, [sPB[7], sBMOD], [sMODS])
            for s in range(2):
                def mv(i):
                    return MODS[:, l, s, i * 8:(i + 1) * 8]
                for (di, gi, sc_i) in ((0, 0, 1), (3, 1, 4), (6, 2, 7)):
                    stt(DER[:, l, s, di, :], mv(sc_i), 1.0, GV[:, l, gi, :], ALU.add, ALU.mult, [sMODS, sGV], [sDER])
                for (di, sh_i) in ((1, 0), (4, 3), (7, 6)):
                    copy(DER[:, l, s, di, :], mv(sh_i), [sMODS], [sDER])
                ts(DER[:, l, s, 2, :], mv(2), 0.5, None, ALU.mult, None, [sMODS], [sDER])
                copy(DER[:, l, s, 5, :], mv(5), [sMODS], [sDER])
                ts(DER[:, l, s, 8, :], mv(8), 0.5, None, ALU.mult, None, [sMODS], [sDER])

        if not P.dry:
            for lyr in range(2):
                WB.convert(lambda cid: cid[0] in ("gu", "d") and cid[1] == 0 and cid[2] == lyr)
                WB.convert(lambda cid: cid[0] in ("k", "v", "pc", "q") and cid[1] == lyr)
                WB.convert(lambda cid: cid[0] in ("gc", "gb") and cid[1] == lyr)
                WB.convert(lambda cid: cid[0] in ("mg", "o") and cid[1] == lyr)
                WB.convert(lambda cid: cid[0] in ("gu", "d") and cid[1] == 1 and cid[2] == lyr)
            BT.convert(lambda cid: True)

        def der(l, s, i, c):
            return DER[:, l, s, i, c:c + 1]

        def xs_(t, c):
            return sX[t * 8 + c]

        norm_done = {"key": None}
        NTb = [(NT, sNTc), (NT2, sNT2c)]

        def norm_mod(l, s, t, ia):
            if norm_done["key"] == (l, s, t, ia):
                norm_done["key"] = None
                return
            for c in range(8):
                norm_sq((l, s, t, ia), c, sq8=True)
            for c in range(8):
                norm_sqmm((l, s, t, ia), c, sq8=True)
            norm_fin()
            for c in range(8):
                norm_chunk((l, s, t, ia), c)

        def sqbuf(c):
            k = c % 3
            if k == 2:
                return SQ2[:, :], sSQ2
            return SCR[k][:, 0:256].bitcast(BF16), sSCR[k]

        def _sq_eng(s):
            if s == 1:
                return ["act", "dve", "pool", "act", "dve", "act", "dve", "pool"]
            return ["act", "dve", "act", "dve", "act", "dve", "act", "dve"]

        def norm_sq(key, c, sq8=False):
            l, s, t, ia = key
            pp, P.phase = P.phase, "norm"
            xin = X[:, c, t * 512:(t + 1) * 512]
            if sq8:
                sqv, ssl = SQ8[:, c, :], sSQ8c[c]
            else:
                sqv, ssl = sqbuf(c)
            e = _sq_eng(s)[c]
            if e == "act":
                act(sqv, xin, AF.Square, [xs_(t, c)], [ssl])
            else:
                tt(sqv, xin, xin, ALU.mult, [xs_(t, c)], [ssl], eng=e)
            P.phase = pp

        def norm_sqmm(key, c, sq8=False):
            pp, P.phase = P.phase, "norm"
            if sq8:
                sqv, ssl = SQ8[:, c, :], sSQ8c[c]
            else:
                sqv, ssl = sqbuf(c)
            mm(PB[0][:, :], ONESB[:, :], sqv, c == 0, c == 7, [ssl, sONES], [sPB[0]])
            P.phase = pp

        def norm_fin():
            pp, P.phase = P.phase, "norm"
            act(RSTD[:, :], PB[0][:, :], AF.Ln, [sPB[0]], [sRSTD], bias=EPS, scale=1.0)
            act(RSTD[:, :], RSTD[:, :], AF.Exp, [sRSTD], [sRSTD], scale=-0.5)
            P.phase = pp

        def norm_chunk(key, c, nb=0):
            l, s, t, ia = key
            pp, P.phase = P.phase, "norm"
            NTx, sNTx = NTb[nb]
            c0 = t * 512
            k = c % 2
            stt(SCR[k][:, 0:512], X[:, c, c0:c0 + 512], der(l, s, ia, c), RSTD[:, :], ALU.mult, ALU.mult,
                [xs_(t, c), sDER, sRSTD], [sSCR[k]])
            act(NTx[:, c, :], SCR[k][:, 0:512], AF.Identity, [sSCR[k], sDER], [sNTx[c]], bias=der(l, s, ia + 1, c), scale=1.0)
            P.phase = pp

        def ffn(l, s, t, which, nxt=None):
            ia = 0 if which == 0 else 6
            c0 = t * 512
            norm_mod(l, s, t, ia)
            P.phase = "ffn.gu"
            for j2 in range(11):
                P.phase = f"ffn.gu.{j2}"
                buf, bs = WB.get([
                    (lambda b: wview(b, 8, 512)[:, :, 0:256], wsrc(wg_d[which][l, :, j2 * 256:(j2 + 1) * 256])),
                    (lambda b: wview(b, 8, 512)[:, :, 256:512], wsrc(wu_d[which][l, :, j2 * 256:(j2 + 1) * 256])),
                ], cid=("gu", which, l, j2), ncols=4096)
                bv = wview(buf, 8, 512)
                for jj in range(2):
                    j = j2 * 2 + jj
                    ig = 1 + (j % 2)
                    iu = 3 + (j % 2)
                    for kc in range(8):
                        mm(PB[ig][:, :], bv[:, kc, jj * 128:(jj + 1) * 128], NT[:, kc, :], kc == 0, kc == 7, [bs, sNTc[kc]], [sPB[ig]])
                    for kc in range(8):
                        mm(PB[iu][:, :], bv[:, kc, 256 + jj * 128:256 + (jj + 1) * 128], NT[:, kc, :], kc == 0, kc == 7, [bs, sNTc[kc]], [sPB[iu]])
                    k = j % 2
                    act(SCR[k][:, 0:512], PB[ig][:, :], AF.Silu, [sPB[ig]], [sSCR[k]])
                    tt(ACTT[:, j, :], SCR[k][:, 0:512], PB[iu][:, :], ALU.mult, [sSCR[k], sPB[iu]], [sACTT])
            for dc in range(8):
                P.phase = f"ffn.down.{dc}"
                if nxt is not None and dc < 3:
                    for c in ((0, 1, 2), (3, 4, 5), (6, 7))[dc]:
                        norm_sq(nxt, c)
                buf, bs = WB.get([(lambda b: wview(b, 22, 128), wsrc(wd_d[which][l, :, dc * 128:(dc + 1) * 128]))], cid=("d", which, l, dc), ncols=2816)
                bv = wview(buf, 22, 128)
                ip = 5 + (dc % 2)
                for kc in range(22):
                    mm(PB[ip][:, :], bv[:, kc, :], ACTT[:, kc, :], kc == 0, kc == 21, [bs, sACTT], [sPB[ip]])
                stt(X[:, dc, c0:c0 + 512], PB[ip][:, :], der(l, s, ia + 2, dc), X[:, dc, c0:c0 + 512], ALU.mult, ALU.add,
                    [sPB[ip], sDER, xs_(t, dc)], [xs_(t, dc)])
                if nxt is not None:
                    if dc < 3:
                        for c in ((0, 1, 2), (3, 4, 5), (6, 7))[dc]:
                            norm_sqmm(nxt, c)
                    if dc == 2:
                        norm_fin()
                    if 3 <= dc <= 6:
                        norm_chunk(nxt, 2 * (dc - 3))
                        norm_chunk(nxt, 2 * (dc - 3) + 1)
            if nxt is not None:
                norm_done["key"] = nxt

        qkc = {"n": 0}

        def qknorm(ps, sps, dst, sdst, gcol):
            k = qkc["n"] % 2
            qkc["n"] += 1
            if k == 0:
                sqv, ssq = SQ2[:, :], sSQ2
                pb, spb = PB[0], sPB[0]
                rs, srs = RSTD[:, :], sRSTD
            else:
                sqv, ssq = SCR[1][:, 0:256].bitcast(BF16), sSCR[1]
                pb, spb = PB[7], sPB[7]
                rs, srs = SCR[0][:, 0:512], sSCR[0]
            act(sqv, ps, AF.Square, [sps], [ssq])
            mm(pb[:, :], BD64[:, :], sqv, True, True, [ssq, sBD64], [spb])
            act(rs, pb[:, :], AF.Ln, [spb], [srs], bias=EPS, scale=1.0)
            act(rs, rs, AF.Exp, [srs], [srs], scale=-0.5)
            stt(dst, ps, gcol, rs, ALU.mult, ALU.mult, [sps, sGQK, srs], [sdst])

        def win(l, c0, n):
            return wsrc(win_d[l, :, c0:c0 + n])

        def upv(buf, grp, t, c, lo, hi):
            if grp == "p":
                v = buf[:, c, 0:4 * 272].rearrange("p (s m) -> p s m", m=272)
                return v[:, 2 * t:2 * t + 2, 8 + lo:8 + hi]
            return buf[:, c, 8 + t * 512 + lo:8 + t * 512 + hi].unsqueeze(1)

        def seg(ap2, grp, n=None):
            if grp == "p":
                return ap2.rearrange("p (s m) -> p s m", m=256)
            return ap2.unsqueeze(1)

        def mix1(l, s, grp, t):
            c0 = t * 512
            norm_mod(l, s, t, 3)
            P.phase = "mix1"
            NT, sNTc = NTb[t % 2]
            ntile_g = 2 if grp == "p" else 5
            nkey = (l, s, t + 1, 3) if t + 1 < ntile_g else None
            if nkey is not None:
                for c in range(8):
                    norm_sq(nkey, c, sq8=True)
            buf, bs = WB.get([(lambda b: wview(b, 8, 512), win(l, 512, 512))], cid=("k", l), ncols=4096)
            bv = wview(buf, 8, 512)
            for pr in range(4):
                ip = 1 + pr
                for kc in range(8):
                    mm(PB[ip][:, :], bv[:, kc, pr * 128:(pr + 1) * 128], NT[:, kc, :], kc == 0, kc == 7, [bs, sNTc[kc]], [sPB[ip]])
            for pr in range(4):
                ip = 1 + pr
                qknorm(PB[ip][:, :], sPB[ip], KT[:, pr, c0:c0 + 512], sKT, GQK[:, l, 1:2])
            if nkey is not None:
                for c in range(8):
                    norm_sqmm(nkey, c, sq8=True)
                norm_fin()
                for c in range(8):
                    norm_chunk(nkey, c, nb=(t + 1) % 2)
                norm_done["key"] = nkey
            if grp == "p":
                P.phase = "mix1.ktm"
                for tb in range(4):
                    ip = 5 + (tb % 2)
                    for kc in range(8):
                        mm(PB[ip][:, :], NT[:, kc, tb * 128:(tb + 1) * 128], bv[:, kc, :], kc == 0, kc == 7, [bs, sNTc[kc]], [sPB[ip]])
                    act(SCR[0][:, 0:512], PB[ip][:, :], AF.Square, [sPB[ip]], [sSCR[0]])
                    P.op("dve", lambda e: e.tensor_reduce(out=SS8[:, :], in_=SCR[0][:, 0:512].rearrange("p (h d) -> p h d", d=64),
                                                         axis=AX.X, op=ALU.add), [sSCR[0]], [sSS8])
                    act(SS8[:, :], SS8[:, :], AF.Ln, [sSS8], [sSS8], bias=EPS, scale=1.0 / 64.0)
                    act(SS8[:, :], SS8[:, :], AF.Exp, [sSS8], [sSS8], scale=-0.5)
                    tt(OUTK[:, :].rearrange("p (h d) -> p h d", d=64), PB[ip][:, :].rearrange("p (h d) -> p h d", d=64),
                       SS8[:, :].unsqueeze(2).broadcast_to([128, 8, 64]), ALU.mult, [sPB[ip], sSS8], [sOUTK])
                    tt(OUTK[:, :].rearrange("p (h d) -> p h d", d=64), OUTK[:, :].rearrange("p (h d) -> p h d", d=64),
                       GKB[:, l, :].unsqueeze(1).broadcast_to([128, 8, 64]), ALU.mult, [sOUTK, sGKB], [sOUTK])
                    r0 = t * 512 + tb * 128
                    dma("sp", nk_d[r0:r0 + 128, l, :], OUTK[:, :], [sOUTK], [new_out()], "ok")
            P.phase = "mix1.v"
            buf, bs = WB.get([(lambda b: wview(b, 8, 512), win(l, 1024, 512))], cid=("v", l), ncols=4096)
            bv = wview(buf, 8, 512)
            for tb in range(4):
                ip = 5 + (tb % 2)
                for kc in range(8):
                    mm(PB[ip][:, :], NT[:, kc, tb * 128:(tb + 1) * 128], bv[:, kc, :], kc == 0, kc == 7, [bs, sNTc[kc]], [sPB[ip]])
                copy(VA[:, t * 4 + tb, :], PB[ip][:, :], [sPB[ip]], [sVA], eng="act")
                if grp == "p":
                    copy(OUTV[:, :], PB[ip][:, :], [sPB[ip]], [sOUTV])
                    r0 = t * 512 + tb * 128
                    dma("sp", nv_d[r0:r0 + 128, l, :], OUTV[:, :], [sOUTV], [new_out()], "ov")
            P.phase = "mix1.pc"
            buf, bs = WB.get([(lambda b: wview(b, 8, 512), win(l, 1536, 512))], cid=("pc", l), ncols=4096)
            bv = wview(buf, 8, 512)
            for c4 in range(4):
                ip = 1 + (c4 % 2)
                for kc in range(8):
                    mm(PB[ip][:, :], bv[:, kc, c4 * 128:(c4 + 1) * 128], NT[:, kc, :], kc == 0, kc == 7, [bs, sNTc[kc]], [sPB[ip]])
                if c4 < 2:
                    copy(upv(UP, grp, t, c4, 0, 512 if grp == "s" else 256), seg(PB[ip][:, :], grp), [sPB[ip]], [sUP], eng="act")
                else:
                    copy(UCV[:, c4 - 2, :], PB[ip][:, :], [sPB[ip]], [sUCV], eng="act")
            P.phase = "mix1.gc"
            buf, bs = WS.get([(lambda b: wview(b, 8, 256), win(l, 2304, 256))], cid=("gc", l), ncols=2048)
            bv = wview(buf, 8, 256)
            for c2 in range(2):
                ip = 3 + (c2 % 2)
                for kc in range(8):
                    mm(PB[ip][:, :], bv[:, kc, c2 * 128:(c2 + 1) * 128], NT[:, kc, :], kc == 0, kc == 7, [bs, sNTc[kc]], [sPB[ip]])
                tt(upv(ZZ, grp, t, c2, 0, 512 if grp == "s" else 256), seg(UCV[:, c2, :], grp), seg(PB[ip][:, :], grp), ALU.mult,
                   [sUCV, sPB[ip]], [sZZ])

        def attn_prompt(l, t):
            c0 = t * 512
            n = 0
            for sq in range(2):
                for h in range(8):
                    pr, po = h // 2, (h % 2) * 64
                    isb = 1 + (n % 3)
                    k = n % 2
                    io, idn = 4 + k, 6 + k
                    n += 1
                    qap = QT[po:po + 64, pr, sq * 256:(sq + 1) * 256]
                    for kb in range(2):
                        kc0 = c0 + sq * 256 + kb * 128
                        mm(PB[isb][:, kb * 256:(kb + 1) * 256], KT[po:po + 64, pr, kc0:kc0 + 128], qap, True, True, [sKT, sQT], [sPB[isb]])
                    act(PT[k][:, :], PB[isb][:, :], AF.Exp, [sPB[isb]], [sPT[k]], scale=0.125)
                    for kb in range(2):
                        mm(PB[io][0:64, 0:256], VA[:, t * 4 + sq * 2 + kb, h * 64:(h + 1) * 64], PT[k][:, kb * 256:(kb + 1) * 256],
                           kb == 0, kb == 1, [sVA, sPT[k]], [sPB[io]])
                    for kb in range(2):
                        mm(PB[idn][0:64, 0:256], ONES64[:, :], PT[k][:, kb * 256:(kb + 1) * 256],
                           kb == 0, kb == 1, [sONES64, sPT[k]], [sPB[idn]])
                    act(RSTD[0:64, 0:256], PB[idn][0:64, 0:256], AF.Ln, [sPB[idn]], [sRSTD])
                    act(RSTD[0:64, 0:256], RSTD[0:64, 0:256], AF.Exp, [sRSTD], [sRSTD], scale=-1.0)
                    tt(AT[po:po + 64, pr, sq * 256:(sq + 1) * 256], PB[io][0:64, 0:256], RSTD[0:64, 0:256], ALU.mult,
                       [sPB[io], sRSTD], [sAT])

        def attn_sample(l, t):
            cnt = {"s": 0, "e": 0}
            pending = []
            for h in range(8):
                pr, po = h // 2, (h % 2) * 64
                sts = []
                prev = None
                for jb in range(2):
                    j = 2 * t + jb
                    typ = 0 if j == 0 else (2 if j == 9 else 1)
                    brow = min(max(4 * j - 4, 0), 28)
                    if prev is None or prev[0] != typ:
                        bt, bts = BT.get([(lambda b: b[:, :], btab_d[l, typ, h, :, :])], barrier=(jb == 0 and h == 0), cid=("bt", l, typ, h), ncols=1536)
                        prev = (typ, bt, bts)
                    sts.append({"jb": jb, "kch0": brow // 2, "bt": prev[1], "bts": prev[2], "io": 4 + jb, "idn": 6 + jb,
                                "qap": QT[po:po + 64, pr, jb * 256:(jb + 1) * 256], "isb": {}})

                def emit_s(st, cp):
                    isb = 1 + (cnt["s"] % 3)
                    cnt["s"] += 1
                    st["isb"][cp] = isb
                    for ii in range(2):
                        if cp < 3:
                            kc0 = (st["kch0"] + cp * 2 + ii) * 128
                            lhs = KT[po:po + 64, pr, kc0:kc0 + 128]
                            rd = [sKT, sQT]
                        else:
                            i = (cp - 3) * 2 + ii
                            lhs = CKT[po:po + 64, pr, i * 128:(i + 1) * 128]
                            rd = [sCKT, sQT]
                        mm(PB[isb][:, ii * 256:(ii + 1) * 256], lhs, st["qap"], True, True, rd, [sPB[isb]])

                def emit_pv(st, cp):
                    isb = st["isb"][cp]
                    k = cnt["e"] % 2
                    cnt["e"] += 1
                    io, idn = st["io"], st["idn"]
                    if cp < 3:
                        stt(SCR[k][:, 0:512], PB[isb][:, :], 0.125, st["bt"][:, cp * 512:(cp + 1) * 512], ALU.mult, ALU.add,
                            [sPB[isb], st["bts"]], [sSCR[k]])
                        act(PT[k][:, :], SCR[k][:, 0:512], AF.Exp, [sSCR[k]], [sPT[k]])
                    else:
                        act(PT[k][:, :], PB[isb][:, :], AF.Exp, [sPB[isb]], [sPT[k]], scale=0.125)
                    for ii in range(2):
                        if cp < 3:
                            lhs = VA[:, st["kch0"] + cp * 2 + ii, h * 64:(h + 1) * 64]
                            rd = [sVA, sPT[k]]
                        else:
                            i = (cp - 3) * 2 + ii
                            lhs = CVA[:, i, h * 64:(h + 1) * 64]
                            rd = [sCVA, sPT[k]]
                        first = (cp == 0 and ii == 0)
                        last = (cp == 4 and ii == 1)
                        mm(PB[io][0:64, 0:256], lhs, PT[k][:, ii * 256:(ii + 1) * 256], first, last, rd, [sPB[io]])
                        mm(PB[idn][0:64, 0:256], ONES64[:, :], PT[k][:, ii * 256:(ii + 1) * 256], first, last,
                           [sONES64, sPT[k]], [sPB[idn]])

                A, B = sts
                emit_s(A, 0)
                emit_s(B, 0)
                for cp in range(5):
                    if cp + 1 < 5:
                        emit_s(A, cp + 1)
                    emit_pv(A, cp)
                    if cp + 1 < 5:
                        emit_s(B, cp + 1)
                    emit_pv(B, cp)

                def mk_norm(st, po=po, pr=pr):
                    def f():
                        jb = st["jb"]
                        io, idn = st["io"], st["idn"]
                        act(SS64[jb][0:64, 0:256], PB[idn][0:64, 0:256], AF.Ln, [sPB[idn]], [sSS64[jb]])
                        act(SS64[jb][0:64, 0:256], SS64[jb][0:64, 0:256], AF.Exp, [sSS64[jb]], [sSS64[jb]], scale=-1.0)
                        tt(AT[po:po + 64, pr, jb * 256:(jb + 1) * 256], PB[io][0:64, 0:256], SS64[jb][0:64, 0:256], ALU.mult,
                           [sPB[io], sSS64[jb]], [sAT])
                    return f
                mk_norm(A)()
                mk_norm(B)()

        def pool_conv(l, grp, t):
            S, n = (2, 256) if grp == "p" else (1, 512)
            m = n + 16
            if grp == "p":
                ktab = 0
            else:
                ktab = 1 if t == 0 else (3 if t == 4 else 2)
            dma("sp", INVC[:, :, :].rearrange("p a b -> p (a b)"), invc_d[ktab], [], [sINVC], "iv")
            buf, bs = WS.get([(lambda b: wview(b, 8, 256), win(l, 2048, 256))], cid=("gb", l), ncols=2048)
            bv = wview(buf, 8, 256)
            for c in range(2):
                ip = 3 + (c % 2)
                for kc in range(8):
                    mm(PB[ip][:, :], bv[:, kc, c * 128:(c + 1) * 128], NT[:, kc, :], kc == 0, kc == 7, [bs, sNTc[kc]], [sPB[ip]])

            def U(c, lo, hi, p0=0, p1=128):
                return upv(UP, grp, t, c, lo, hi)[p0:p1]

            def tv(h2, lo, hi, p0=0, p1=128):
                return h2.rearrange("p (s m) -> p s m", m=m)[p0:p1, :, 8 + lo:8 + hi]

            def TAv(c, lo, hi, p0=0, p1=128):
                return tv(TA[:, c, 0:S * m], lo, hi, p0, p1)

            def TBv(lo, hi, p0=0, p1=128):
                return tv(SCR[0][:, 0:S * m], lo, hi, p0, p1)

            def TCv(lo, hi, p0=0, p1=128):
                return tv(SCR[1][:, 0:S * m], lo, hi, p0, p1)

            def SUMv(c, p0=0, p1=128):
                return SUMT[p0:p1, c, :].rearrange("p (s m) -> p s m", m=n)

            add = ALU.add
            tt(TAv(0, -7, n + 7, 64, 128), U(0, -8, n + 6, 64, 128), U(0, -7, n + 7, 64, 128), add, [sUP], [sTA])
            tt(TAv(1, -7, n + 7), U(1, -8, n + 6), U(1, -7, n + 7), add, [sUP], [sTA])
            tt(SUMv(0, 0, 64), U(0, -1, n - 1, 0, 64), U(0, 0, n, 0, 64), add, [sUP], [sSUMT])
            tt(SUMv(0, 64, 128), TAv(0, -1, n - 1, 64, 128), TAv(0, 1, n + 1, 64, 128), add, [sTA], [sSUMT])
            tt(TBv(-6, n + 6), TAv(1, -7, n + 5), TAv(1, -5, n + 7), add, [sTA], [sSCR[0]])
            tt(SUMv(1, 0, 64), TBv(-2, n - 2, 0, 64), TBv(2, n + 2, 0, 64), add, [sSCR[0]], [sSUMT])
            tt(TCv(-4, n + 4, 64, 128), TBv(-6, n + 2, 64, 128), TBv(-2, n + 6, 64, 128), add, [sSCR[0]], [sSCR[1]])
            tt(SUMv(1, 64, 128), TCv(-4, n - 4, 64, 128), TCv(4, n + 4, 64, 128), add, [sSCR[1]], [sSUMT])
            for c in range(2):
                tt(SUMT[:, c, :], SUMT[:, c, :], INVC[:, c, :], ALU.mult, [sSUMT, sINVC], [sSUMT])
                tt(seg(DD[:, c, :], grp), SUMv(c), U(c, 0, n), ALU.subtract, [sSUMT, sUP], [sDD])
            for c in range(2):
                ip = 1 + (c % 2)
                mm(PB[ip][:, :], WPB[:, l, c, :], DD[:, c, :], True, True, [sWPB, sDD], [sPB[ip]])
                act(PL[:, c, :], PB[ip][:, :], AF.Identity, [sPB[ip], sPCV], [sPL], scale=PCV[:, l, 0, c:c + 1])

            def Zv(c, lo, hi):
                return upv(ZZ, grp, t, c, lo, hi)

            for c in range(2):
                acc = SUMv(c)
                ts(acc, Zv(c, -1, n - 1), PCV[:, l, 1, c:c + 1], PCV[:, l, 4, c:c + 1], ALU.mult, ALU.add, [sZZ, sPCV, sSUMT], [sSUMT])
                stt(acc, Zv(c, 0, n), PCV[:, l, 2, c:c + 1], acc, ALU.mult, ALU.add, [sZZ, sPCV, sSUMT], [sSUMT])
                stt(acc, Zv(c, 1, n + 1), PCV[:, l, 3, c:c + 1], acc, ALU.mult, ALU.add, [sZZ, sPCV, sSUMT], [sSUMT])
                ip = 3 + (c % 2)
                tt(CVT[:, c, :], SUMT[:, c, :], PB[ip][:, :], ALU.mult, [sSUMT, sPB[ip]], [sCVT])

        def mix2(l, s, grp, t):
            c0 = t * 512
            norm_mod(l, s, t, 3)
            if grp == "s":
                dma("pool", CKT[:, :, :], ckT_d[l].rearrange("a p k -> p a k"), [], [sCKT], "ck")
                dma("pool", CVA[:, :, :], cvv_d[l].rearrange("a p k -> p a k"), [], [sCVA], "cv")
            P.phase = "mix2.q"
            buf, bs = WB.get([(lambda b: wview(b, 8, 512), win(l, 0, 512))], cid=("q", l), ncols=4096)
            bv = wview(buf, 8, 512)
            for pr in range(4):
                ip = 1 + pr
                for kc in range(8):
                    mm(PB[ip][:, :], bv[:, kc, pr * 128:(pr + 1) * 128], NT[:, kc, :], kc == 0, kc == 7, [bs, sNTc[kc]], [sPB[ip]])
            for pr in range(4):
                ip = 1 + pr
                qknorm(PB[ip][:, :], sPB[ip], QT[:, pr, :], sQT, GQK[:, l, 0:1])
            P.phase = "attn." + grp
            if grp == "p":
                attn_prompt(l, t)
            else:
                attn_sample(l, t)
            P.phase = "poolconv"
            pool_conv(l, grp, t)
            P.phase = "merge"
            if DEBUG_LIMIT is not None and grp == "p" and t == 0 and l == 0:
                dma("pool", dbg_d[:, 0:4, :], AT[:, :, :], [sAT], [new_out()], "dbg")
                dma("pool", dbg_d[:, 4:6, :], PL[:, :, :], [sPL], [new_out()], "dbg")
                dma("pool", dbg_d[:, 6:8, :], CVT[:, :, :], [sCVT], [new_out()], "dbg")
            for br, (gc0, wbr, kcn, SRC, ssrc) in enumerate(((2560, wba_d, 4, AT, sAT), (3584, wbp_d, 2, PL, sPL), (4608, wbc_d, 2, CVT, sCVT))):
                for qt in range(4):
                    P.phase = f"merge.{br}.{qt}"
                    nb_cols = kcn * 256
                    gbuf, gs = WB.get([
                        (lambda b: b[:, 0:2048].rearrange("p (k c) -> p k c", c=256), win(l, gc0 + qt * 256, 256)),
                        (lambda b, kcn=kcn: b[:, 2048:2048 + kcn * 256].rearrange("p (k c) -> p k c", c=256),
                         wsrc(wbr[l, :, qt * 256:(qt + 1) * 256])),
                    ], cid=("mg", l, br, qt), ncols=2048 + nb_cols)
                    gv = gbuf[:, 0:2048].rearrange("p (k c) -> p k c", c=256)
                    bbv = gbuf[:, 2048:2048 + nb_cols].rearrange("p (k c) -> p k c", c=256)
                    bbs = gs
                    for d2 in range(2):
                        dc = qt * 2 + d2
                        ig, ib = 1 + (dc % 2), 3 + (dc % 2)
                        for kc in range(8):
                            mm(PB[ig][:, :], gv[:, kc, d2 * 128:(d2 + 1) * 128], NT[:, kc, :], kc == 0, kc == 7, [gs, sNTc[kc]], [sPB[ig]])
                        for kc in range(kcn):
                            mm(PB[ib][:, :], bbv[:, kc, d2 * 128:(d2 + 1) * 128], SRC[:, kc, :], kc == 0, kc == kcn - 1, [bbs, ssrc], [sPB[ib]])
                        k = dc % 2
                        act(SCR[k][:, 0:512], PB[ig][:, :], AF.Sigmoid, [sPB[ig]], [sSCR[k]])
                        if br == 0:
                            tt(MG[:, dc, :], SCR[k][:, 0:512], PB[ib][:, :], ALU.mult, [sSCR[k], sPB[ib]], [sMG[dc]])
                        else:
                            tt(SCR[k][:, 0:512], SCR[k][:, 0:512], PB[ib][:, :], ALU.mult, [sSCR[k], sPB[ib]], [sSCR[k]])
                            tt(MG[:, dc, :], MG[:, dc, :], SCR[k][:, 0:512], ALU.add, [sMG[dc], sSCR[k]], [sMG[dc]])
            P.phase = "outproj"
            for dc in range(8):
                copy(NT[:, dc, :], MG[:, dc, :], [sMG[dc]], [sNTc[dc]], eng="act")
            for half in range(2):
                buf, bs = WB.get([(lambda b: wview(b, 8, 512), wsrc(wout_d[l, :, half * 512:(half + 1) * 512]))], cid=("o", l, half), ncols=4096)
                bv = wview(buf, 8, 512)
                for d4 in range(4):
                    dc = half * 4 + d4
                    ip = 5 + (d4 % 2)
                    for kc in range(8):
                        mm(PB[ip][:, :], bv[:, kc, d4 * 128:(d4 + 1) * 128], NT[:, kc, :], kc == 0, kc == 7, [bs, sNTc[kc]], [sPB[ip]])
                    stt(X[:, dc, c0:c0 + 512], PB[ip][:, :], der(l, s, 5, dc), X[:, dc, c0:c0 + 512], ALU.mult, ALU.add,
                        [sPB[ip], sDER, xs_(t, dc)], [xs_(t, dc)])

        for grp, ntile, x_d, y_d, s in (("p", 2, xp_d, yp_d, 0), ("s", 5, xs_d, ys_d, 1)):
            if grp not in DEBUG_GROUPS:
                continue
            cur["ntile"], cur["y_d"] = ntile, y_d
            for t in range(ntile):
                dma("sp", X[:, :, t * 512:(t + 1) * 512], x_d[:, :, t * 512:(t + 1) * 512], [], [sX[t * 8 + c] for c in range(8)], f"x{t}")
            memset(UP[:, :, :], 0.0, [sUP])
            memset(ZZ[:, :, :], 0.0, [sZZ])
            for l in range(2):
                stage()
                for t in range(ntile):
                    ffn(l, s, t, 0, nxt=((l, s, t + 1, 0) if t + 1 < ntile else (l, s, 0, 3)))
                stage()
                for t in range(ntile):
                    mix1(l, s, grp, t)
                stage()
                for t in range(ntile):
                    mix2(l, s, grp, t)
                stage()
                for t in range(ntile):
                    ffn(l, s, t, 1, nxt=((l, s, t + 1, 6) if t + 1 < ntile else ((l + 1, s, 0, 0) if l == 0 else None)))
            for t in range(ntile):
                dma("sp", y_d[:, :, t * 512:(t + 1) * 512], X[:, :, t * 512:(t + 1) * 512], [sX[t * 8 + c] for c in range(8)], [new_out()], f"x{t}")
        P.op("sp", lambda e: e.nop(), list(out_slots), [])

    def new_out():
        sl = Slot("o")
        out_slots.append(sl)
        return sl

    P.dry = True
    body()
    P.dry = False
    for st in (WB, BT):
        st.reset()
    body()
    P.emit()
    _CACHE["pe_tags"] = [r["tag"] for r in P.ops["pe"]]
    return nc


_CACHE = {}
DEBUG_LIMIT = None
DEBUG_GROUPS = "ps"


class StopBody(Exception):
    pass


def _host_tables(rpb):
    key = np.arange(768)
    krel = key // 64
    kcol = key % 64
    q = np.arange(256)
    qr = q // 64
    qc = q % 64
    cs = np.clip(qc - 8, 0, 48)
    out = np.empty((2, 3, 8, 768, 256), np.float32)
    for typ in range(3):
        qrr = {0: qr, 1: 4 + qr, 2: 8 + qr}[typ]
        ws = {0: 0 * qr, 1: qr, 2: 4 + 0 * qr}[typ]
        valid = ((krel[:, None] >= ws[None, :]) & (krel[:, None] < ws[None, :] + 8)
                 & (kcol[:, None] >= cs[None, :]) & (kcol[:, None] < cs[None, :] + 16))
        ro = np.clip(krel[:, None] - qrr[None, :] + 7, 0, 14)
        co = np.clip(kcol[:, None] - qc[None, :] + 15, 0, 30)
        vals = rpb[:, :, ro, co]
        out[:, typ] = np.where(valid[None, None], vals, np.float32(NEG))
    out = out.reshape(2, 3, 8, 6, 128, 256).transpose(0, 1, 2, 4, 3, 5)
    return np.ascontiguousarray(out).reshape(2, 3, 8, 128, 6 * 256)


def _invc_tables():
    tabs = np.empty((4, 128, 2, 512), np.float32)
    n = np.arange(512)
    for k in range(4):
        if k == 0:
            pos, Lq = n % 256, 256
        elif k == 1:
            pos, Lq = n, 2560
        elif k == 2:
            pos, Lq = n + 1024, 2560
        else:
            pos, Lq = n + 2048, 2560
        for c in range(2):
            for hf in range(2):
                w = POOL_WINDOWS[2 * c + hf]
                lo = np.clip(pos - w // 2, 0, Lq)
                hi = np.clip(pos - w // 2 + w, 0, Lq)
                tabs[k, hf * 64:(hf + 1) * 64, c, :] = (np.float32(1.0) / (hi - lo).astype(np.float32))[None, :]
    return tabs.reshape(4, 128, 1024)


def kernel(x_prompt, x_sample, cache_k, cache_v, c, c_ctx, w_mod, b_mod,
           g_ffn1, w_ffn1_gate, w_ffn1_up, w_ffn1_down, g_mix, w_in, g_q, g_k, rpb,
           w_pool, pool_scale, w_conv, b_conv, w_br_attn, w_br_pool, w_br_conv, w_out,
           g_ffn2, w_ffn2_gate, w_ffn2_up, w_ffn2_down):
    f = lambda a: np.ascontiguousarray(np.asarray(a, dtype=np.float32))
    x_prompt, x_sample, cache_k, cache_v = f(x_prompt), f(x_sample), f(cache_k), f(cache_v)
    c, c_ctx = f(c), f(c_ctx)
    if "nc" not in _CACHE:
        _CACHE["nc"] = build_program()
    nc = _CACHE["nc"]

    def fm(v):
        return v.reshape(8, 128).T

    bmod = f(b_mod).reshape(2, 72, 128).transpose(2, 0, 1).reshape(128, 144)
    gv = np.stack([np.stack([fm(f(g)[l]) for g in (g_ffn1, g_mix, g_ffn2)], axis=1) for l in range(2)], axis=1)
    p64 = np.arange(128) % 64
    gqk = np.stack([np.stack([f(g_q)[l][p64], f(g_k)[l][p64]], axis=1) for l in range(2)], axis=1)
    gkb = np.broadcast_to(f(g_k)[None], (128, 2, 64))
    pcv = np.empty((128, 2, 5, 2), np.float32)
    for l in range(2):
        pcv[:, l, 0, :] = f(pool_scale)[l].reshape(2, 128).T
        for k in range(3):
            pcv[:, l, 1 + k, :] = f(w_conv)[l, k].reshape(2, 128).T
        pcv[:, l, 4, :] = f(b_conv)[l].reshape(2, 128).T
    wpbd = np.zeros((128, 2, 2, 128), np.float32)
    wp = f(w_pool)
    for l in range(2):
        for cchunk in range(2):
            wpbd[0:64, l, cchunk, 0:64] = wp[l, 2 * cchunk]
            wpbd[64:128, l, cchunk, 64:128] = wp[l, 2 * cchunk + 1]
    btab = _host_tables(f(rpb))
    invc = _invc_tables()
    shared = {
        "w_mod": f(w_mod), "bmod": f(bmod), "gv": f(gv.reshape(128, 48)),
        "w_g1": f(w_ffn1_gate), "w_u1": f(w_ffn1_up), "w_d1": f(w_ffn1_down),
        "w_g2": f(w_ffn2_gate), "w_u2": f(w_ffn2_up), "w_d2": f(w_ffn2_down),
        "w_in": f(w_in), "gqk": f(gqk.reshape(128, 4)), "gkb": f(gkb.reshape(128, 128)),
        "btab": btab, "wpbd": f(wpbd.reshape(128, 512)), "pcv": f(pcv.reshape(128, 20)),
        "w_ba": f(w_br_attn), "w_bp": f(w_br_pool), "w_bc": f(w_br_conv), "w_out": f(w_out),
        "invc": invc,
    }
    in_maps = []
    for core in range(8):
        b, half = core // 2, core % 2
        r0 = 0 if half == 0 else 24
        xs = x_sample[b, r0 * 64:r0 * 64 + 2560].reshape(2560, 8, 128).transpose(2, 1, 0)
        xp = x_prompt[4 * core:4 * core + 4].reshape(1024, 8, 128).transpose(2, 1, 0)
        cc = np.stack([fm(c_ctx), fm(c[b])], axis=2).reshape(128, 16)
        ckT = cache_k[b].reshape(2, 4, 2, 512, 64).transpose(0, 1, 2, 4, 3).reshape(2, 4, 128, 512)
        cvv = cache_v[b].reshape(2, 8, 4, 128, 64).transpose(0, 2, 3, 1, 4).reshape(2, 4, 128, 512)
        m = dict(shared)
        m.update({"xs": f(xs), "xp": f(xp), "cc": f(cc), "ckT": f(ckT), "cvv": f(cvv)})
        in_maps.append(m)
    res = run_bass_kernel_spmd(nc, in_maps, core_ids=list(range(8)))
    y_prompt = np.empty((32, 256, 1024), np.float32)
    y_sample = np.empty((4, 4096, 1024), np.float32)
    new_k = np.empty((32, 2, 8, 256, 64), np.float32)
    new_v = np.empty((32, 2, 8, 256, 64), np.float32)
    for core in range(8):
        r = res.results[core]
        b, half = core // 2, core % 2
        yp = np.asarray(r["yp"]).transpose(2, 1, 0).reshape(4, 256, 1024)
        y_prompt[4 * core:4 * core + 4] = yp
        ys = np.asarray(r["ys"]).transpose(2, 1, 0).reshape(2560, 1024)
        if half == 0:
            y_sample[b, 0:2048] = ys[0:2048]
        else:
            y_sample[b, 2048:4096] = ys[512:2560]
        nk = np.asarray(r["nk"]).reshape(4, 256, 2, 8, 64).transpose(0, 2, 3, 1, 4)
        nv = np.asarray(r["nv"]).reshape(4, 256, 2, 8, 64).transpose(0, 2, 3, 1, 4)
        new_k[4 * core:4 * core + 4] = nk
        new_v[4 * core:4 * core + 4] = nv
    return (y_prompt, y_sample, new_k, new_v)
```

```python
import numpy as np
from contextlib import ExitStack
import concourse.bass as bass
import concourse.mybir as mybir
from concourse.bass_utils import run_bass_kernel_spmd

F32 = mybir.dt.float32
BF16 = mybir.dt.bfloat16
AF = mybir.ActivationFunctionType
ALU = mybir.AluOpType
AX = mybir.AxisListType

ENGS = ("pe", "act", "dve", "pool", "sp")
SB_BASE = 16512
SB_LIMIT = 229344
EPS = 1e-6
NEG = -30000.0
POOL_WINDOWS = (2, 4, 8, 16)


class Slot:
    __slots__ = ("name", "lw", "rd", "conf", "excl")

    def __init__(self, name, excl=False):
        self.name = name
        self.lw = None
        self.rd = {}
        self.conf = []
        self.excl = excl


class Prog:
    def __init__(self, nc):
        self.nc = nc
        self.ops = {e: [] for e in ENGS}
        self.seen = {e: {} for e in ENGS}
        self.dma_cnt = {}
        self.dry = False
        self.phase = ""
        self.sb_off = SB_BASE
        self.sb_slots = []

    def sb(self, name, shape, dtype, at=None, nslots=1):
        esz = 4 if dtype == F32 else 2
        nbytes = int(np.prod(shape[1:])) * esz
        if at is None:
            off = (self.sb_off + 31) // 32 * 32
            self.sb_off = off + nbytes
            assert self.sb_off <= SB_LIMIT, (name, self.sb_off)
        else:
            off = at
            assert off % 32 == 0 and off + nbytes <= SB_LIMIT, (name, off, nbytes)
        h = self.nc.alloc_sbuf_tensor_at(name, list(shape), dtype, offset=off)
        slots = [Slot(f"{name}.{i}") for i in range(nslots)]
        for (lo, hi, sl) in self.sb_slots:
            if lo < off + nbytes and off < hi:
                for a in sl:
                    for b in slots:
                        a.conf.append(b)
                        b.conf.append(a)
        self.sb_slots.append((off, off + nbytes, slots))
        return h, slots

    def _record(self, eng, fn, reads, writes, dma_key=None):
        if self.dry:
            return None
        deps = {}

        def add(tok):
            if tok is None:
                return
            p, v = tok
            if deps.get(p, -1) < v:
                deps[p] = v

        for s in reads:
            add(s.lw)
            if s.excl:
                for t in s.rd.values():
                    if t[0] != eng:
                        add(t)
            for c in s.conf:
                add(c.lw)
        for s in writes:
            add(s.lw)
            for t in s.rd.values():
                add(t)
            for c in s.conf:
                add(c.lw)
                for t in c.rd.values():
                    add(t)
        waits = []
        seen = self.seen[eng]
        for p, v in deps.items():
            if p == "pe" and eng == "pe":
                continue
            if seen.get(p, -1) >= v:
                continue
            seen[p] = v
            waits.append((p, v))
            if not p.startswith("#"):
                self.ops[p][v]["signal"] = True
        idx = len(self.ops[eng])
        rec = {"fn": fn, "waits": waits, "signal": False, "dma": dma_key, "tag": self.phase}
        self.ops[eng].append(rec)
        if dma_key is not None:
            cnt = self.dma_cnt.get(dma_key, 0) + 16
            self.dma_cnt[dma_key] = cnt
            tok = ("#" + dma_key, cnt)
        else:
            tok = (eng, idx)
        for s in reads:
            s.rd[tok[0]] = tok
        for s in writes:
            s.lw = tok
            s.rd = {}
        return tok

    def op(self, eng, fn, reads=(), writes=()):
        return self._record(eng, fn, list(reads), list(writes))

    def dma(self, queue, fn, reads, writes, key):
        return self._record(queue, fn, list(reads), list(writes), dma_key=key)

    def emit(self):
        nc = self.nc
        with ExitStack() as es:
            sem = {e: es.enter_context(nc.semaphore("s_" + e)) for e in ENGS}
            dsem = {"#" + k: es.enter_context(nc.semaphore("d_" + k)) for k in self.dma_cnt}
            for e in ENGS:
                n = 0
                for rec in self.ops[e]:
                    if rec["dma"] is None and rec["signal"]:
                        n += 1
                        rec["seq"] = n
            ops = self.ops

            def run(e, eng):
                for rec in ops[e]:
                    for (p, v) in rec["waits"]:
                        if p.startswith("#"):
                            eng.wait_ge(dsem[p], v)
                        else:
                            eng.wait_ge(sem[p], ops[p][v]["seq"])
                    inst = rec["fn"](eng)
                    if rec["dma"] is not None:
                        inst.then_inc(dsem["#" + rec["dma"]], 16)
                    elif rec["signal"]:
                        inst.then_inc(sem[e], 1)

            with nc.Block() as block:
                @block.tensor
                def _(eng):
                    run("pe", eng)

                @block.scalar
                def _(eng):
                    run("act", eng)

                @block.vector
                def _(eng):
                    run("dve", eng)

                @block.gpsimd
                def _(eng):
                    run("pool", eng)

                @block.sync
                def _(eng):
                    run("sp", eng)


class Stream:
    NCONV = 16

    def __init__(self, P, name, queue, bufs, slots, cache=None):
        self.P, self.name, self.queue = P, name, queue
        self.bufs, self.slots = bufs, slots
        self.plan = []
        self.i = 0
        self.issued = 0
        self.cache = cache
        self.cmap = {}
        self.nconv = 0
        self.kslots = {}

    def reset(self):
        self.i = 0
        self.issued = 0
        self.cmap = {}

    def convert(self, pred):
        mk = lambda d, s: (lambda e: e.dma_start(out=d, in_=s))
        for (parts, barrier, cid, ncols) in self.plan:
            if cid is None or self.cache is None or cid in self.cmap or not pred(cid):
                continue
            idx = len(self.cmap)
            sl = Slot(f"{self.name}c{idx}")
            self.cmap[cid] = (idx, sl)
            key = f"{self.name}v{self.nconv % self.NCONV}"
            self.nconv += 1
            ks = self.kslots.setdefault(key, Slot(key))
            for (dstf, src) in parts:
                dst = dstf(self.cache[idx])
                self.P.dma("pool", mk(dst, src), [], [sl, ks], key)

    def _issue(self, k):
        nb = len(self.bufs)
        b = k % nb
        parts, barrier, cid, ncols = self.plan[k]
        mk = lambda d, s: (lambda e: e.dma_start(out=d, in_=s))
        if cid is None or self.cache is None:
            for (dstf, src) in parts:
                dst = dstf(self.bufs[b])
                self.P.dma(self.queue, mk(dst, src), [], [self.slots[b]], f"{self.name}{b}")
        else:
            idx, sl = self.cmap[cid]
            self.P.dma("sp", mk(self.bufs[b][:, 0:ncols], self.cache[idx, :, 0:ncols]), [sl], [self.slots[b]], f"{self.name}h{b}")

    def get(self, parts, barrier=False, cid=None, ncols=None, hold=0):
        k = self.i
        self.i += 1
        nb = len(self.bufs)
        if self.P.dry:
            self.plan.append((parts, barrier, cid, ncols))
            return self.bufs[k % nb], self.slots[k % nb]
        while self.issued < min(len(self.plan), k + nb - hold):
            if self.issued > k and self.plan[self.issued][1]:
                break
            self._issue(self.issued)
            self.issued += 1
        return self.bufs[k % nb], self.slots[k % nb]


def build_program():
    nc = bass.Bass("TRN2", target_bir_lowering=False)
    P = Prog(nc)

    def din(name, shape):
        return nc.dram_tensor(name, list(shape), F32, kind="ExternalInput").ap()

    def dout(name, shape):
        return nc.dram_tensor(name, list(shape), F32, kind="ExternalOutput").ap()

    xp_d = din("xp", [128, 8, 1024])
    xs_d = din("xs", [128, 8, 2560])
    cc_d = din("cc", [128, 16])
    wmod_d = din("w_mod", [2, 1024, 9216])
    bmod_d = din("bmod", [128, 2 * 72])
    gv_d = din("gv", [128, 2 * 3 * 8])
    wg_d = [din("w_g1", [2, 1024, 2816]), din("w_g2", [2, 1024, 2816])]
    wu_d = [din("w_u1", [2, 1024, 2816]), din("w_u2", [2, 1024, 2816])]
    wd_d = [din("w_d1", [2, 2816, 1024]), din("w_d2", [2, 2816, 1024])]
    win_d = din("w_in", [2, 1024, 5632])
    gqk_d = din("gqk", [128, 4])
    gkb_d = din("gkb", [128, 2 * 64])
    btab_d = din("btab", [2, 3, 8, 128, 6 * 256])
    wpbd_d = din("wpbd", [128, 2 * 2 * 128])
    pcv_d = din("pcv", [128, 2 * 5 * 2])
    wba_d = din("w_ba", [2, 512, 1024])
    wbp_d = din("w_bp", [2, 256, 1024])
    wbc_d = din("w_bc", [2, 256, 1024])
    wout_d = din("w_out", [2, 1024, 1024])
    ckT_d = din("ckT", [2, 4, 128, 512])
    cvv_d = din("cvv", [2, 4, 128, 512])
    invc_d = din("invc", [4, 128, 2 * 512])
    yp_d = dout("yp", [128, 8, 1024])
    ys_d = dout("ys", [128, 8, 2560])
    nk_d = dout("nk", [1024, 2, 512])
    nv_d = dout("nv", [1024, 2, 512])
    out_slots = []
    dbg_d = dout("dbg", [128, 8, 512]) if DEBUG_LIMIT is not None else None

    X, sX = P.sb("X", [128, 8, 2560], F32, nslots=40)
    KT, (sKT,) = P.sb("KT", [128, 4, 2560], BF16)
    VA, (sVA,) = P.sb("VA", [128, 20, 512], BF16)
    UP, (sUP,) = P.sb("UP", [128, 2, 2576], BF16)
    ZZ, (sZZ,) = P.sb("ZZ", [128, 2, 2576], BF16)
    WBb, WBs = [], []
    for i in range(3):
        h, (s,) = P.sb(f"WB{i}", [128, 4096], BF16)
        WBb.append(h)
        WBs.append(s)
    ONESB, (sONES,) = P.sb("ONESB", [128, 128], BF16)
    BD64, (sBD64,) = P.sb("BD64", [128, 128], BF16)
    ONES64, (sONES64,) = P.sb("ONES64", [128, 64], BF16)
    DER, (sDER,) = P.sb("DER", [128, 2, 2, 9, 8], F32)
    GQK, (sGQK,) = P.sb("GQK", [128, 2, 2], F32)
    GKB, (sGKB,) = P.sb("GKB", [128, 2, 64], F32)
    PCV, (sPCV,) = P.sb("PCV", [128, 2, 5, 2], F32)
    WPB, (sWPB,) = P.sb("WPB", [128, 2, 2, 128], BF16)
    SS8, (sSS8,) = P.sb("SS8", [128, 8], F32)
    SCR, sSCR = [], []
    for i in range(2):
        h, (s,) = P.sb(f"SCR{i}", [128, 544], F32)
        SCR.append(h)
        sSCR.append(s)
    RSTD, (sRSTD,) = P.sb("RSTD", [128, 512], F32)
    SQ2, (sSQ2,) = P.sb("SQ2", [128, 512], BF16)
    SS64 = [RSTD[:, 0:256], RSTD[:, 256:512]]
    sSS64 = [sRSTD, sRSTD]
    U0 = (P.sb_off + 31) // 32 * 32
    NT, sNTc = P.sb("NT", [128, 8, 512], BF16, at=U0, nslots=8)
    M0 = U0 + 8192
    K = 1024
    CC, (sCC,) = P.sb("CC", [128, 16], F32, at=M0)
    SCB, (sSCB,) = P.sb("SCB", [128, 8, 2], BF16, at=M0 + 64)
    BMOD, (sBMOD,) = P.sb("BMOD", [128, 2, 72], F32, at=M0 + 128)
    GV, (sGV,) = P.sb("GV", [128, 2, 3, 8], F32, at=M0 + 128 + 576)
    MODS, (sMODS,) = P.sb("MODS", [128, 2, 2, 72], F32, at=M0 + 1024)
    ACTT, (sACTT,) = P.sb("ACTT", [128, 22, 512], BF16, at=M0)
    SQ, sSQc = P.sb("SQ", [128, 8, 512], BF16, at=M0, nslots=8)
    NT2, sNT2c = P.sb("NT2", [128, 8, 512], BF16, at=M0, nslots=8)
    SQ8, sSQ8c = P.sb("SQ8", [128, 8, 512], BF16, at=M0 + 16 * K, nslots=8)
    OUTK, (sOUTK,) = P.sb("OUTK", [128, 512], F32, at=M0 + 8 * K)
    OUTV, (sOUTV,) = P.sb("OUTV", [128, 512], F32, at=M0 + 10 * K)
    UCV, (sUCV,) = P.sb("UCV", [128, 2, 512], F32, at=M0 + 12 * K)
    QT, (sQT,) = P.sb("QT", [128, 4, 512], BF16, at=M0)
    AT, (sAT,) = P.sb("AT", [128, 4, 512], BF16, at=M0 + 4 * K)
    PT, sPT = [], []
    for i in range(2):
        h, (s,) = P.sb(f"PT{i}", [128, 512], BF16, at=M0 + 8 * K + i * K)
        PT.append(h)
        sPT.append(s)
    BTb, BTs_ = [], []
    for i in range(2):
        h, (s,) = P.sb(f"BT{i}", [128, 1536], BF16, at=M0 + 10 * K + i * 3 * K)
        BTb.append(h)
        BTs_.append(s)
    CKT, (sCKT,) = P.sb("CKT", [128, 4, 512], BF16, at=M0 + 16 * K)
    CVA, (sCVA,) = P.sb("CVA", [128, 4, 512], BF16, at=M0 + 20 * K)
    PL, (sPL,) = P.sb("PL", [128, 2, 512], BF16, at=M0)
    CVT, (sCVT,) = P.sb("CVT", [128, 2, 512], BF16, at=M0 + 2 * K)
    TA, (sTA,) = P.sb("TA", [128, 2, 544], F32, at=M0 + 8 * K)
    SUMT, (sSUMT,) = P.sb("SUMT", [128, 2, 512], F32, at=M0 + 8 * K + 4352)
    INVC, (sINVC,) = P.sb("INVC", [128, 2, 512], F32, at=M0 + 8 * K + 4352 + 4096)
    DD, (sDD,) = P.sb("DD", [128, 2, 512], BF16, at=M0 + 8 * K + 4352 + 8192)
    MG, sMG = P.sb("MG", [128, 8, 512], F32, at=M0 + 8 * K, nslots=8)
    assert M0 + 24 * K <= SB_LIMIT, (M0 + 24 * K, SB_LIMIT)

    PB, sPB = [], []
    for i in range(8):
        PB.append(nc.alloc_psum_tensor(f"PB{i}", [128, 512], F32))
        sPB.append(Slot(f"PB{i}", excl=True))

    wbc_h = nc.dram_tensor("wb_cache", [116, 128, 4096], BF16).ap()
    WB = Stream(P, "wb", "pool", WBb, WBs, cache=wbc_h)
    WS = WB
    btc_h = nc.dram_tensor("bt_cache", [48, 128, 1536], BF16).ap()
    BT = Stream(P, "bt", "pool", BTb, BTs_, cache=btc_h)

    def mm(out, lhsT, rhs, start, stop, reads, writes):
        P.op("pe", lambda e: e.matmul(out, lhsT=lhsT, rhs=rhs, start=start, stop=stop), reads, writes)

    def act(out, in_, func, reads, writes, bias=None, scale=None):
        kw = {}
        if bias is not None:
            kw["bias"] = bias
        if scale is not None:
            kw["scale"] = scale
        P.op("act", lambda e: e.activation(out=out, in_=in_, func=func, **kw), reads, writes)

    def tt(out, in0, in1, op, reads, writes, eng="dve"):
        P.op(eng, lambda e: e.tensor_tensor(out=out, in0=in0, in1=in1, op=op), reads, writes)

    def ts(out, in0, s1, s2, op0, op1, reads, writes, eng="dve"):
        if op1 is None:
            P.op(eng, lambda e: e.tensor_scalar(out=out, in0=in0, scalar1=s1, scalar2=None, op0=op0), reads, writes)
        else:
            P.op(eng, lambda e: e.tensor_scalar(out=out, in0=in0, scalar1=s1, scalar2=s2, op0=op0, op1=op1), reads, writes)

    def stt(out, in0, scalar, in1, op0, op1, reads, writes, eng="dve"):
        P.op(eng, lambda e: e.scalar_tensor_tensor(out=out, in0=in0, scalar=scalar, in1=in1, op0=op0, op1=op1), reads, writes)

    def recip(out, in_, reads, writes):
        P.op("dve", lambda e: e.reciprocal(out=out, in_=in_), reads, writes)

    def copy(out, in_, reads, writes, eng="dve"):
        if eng == "act":
            P.op("act", lambda e: e.copy(out=out, in_=in_), reads, writes)
        else:
            P.op(eng, lambda e: e.tensor_copy(out=out, in_=in_), reads, writes)

    def memset(ap, val, writes, eng="dve"):
        P.op(eng, lambda e: e.memset(ap, val), [], writes)

    def dma(queue, out, in_, reads, writes, key):
        P.dma(queue, lambda e: e.dma_start(out=out, in_=in_), reads, writes, key)

    def wview(buf, kc, n):
        return buf[:, 0:kc * n].rearrange("p (k c) -> p k c", c=n)

    def wsrc(d_ap2):
        return d_ap2.rearrange("(kc p) c -> p kc c", p=128)

    psc = {"n": 0}

    def rot(lst):
        psc["n"] += 1
        return lst[psc["n"] % len(lst)]

    stg = {"n": 0}
    cur = {}

    def stage():
        stg["n"] += 1
        if DEBUG_LIMIT is not None and stg["n"] > DEBUG_LIMIT:
            raise StopBody()

    def body():
        try:
            body_inner()
        except StopBody:
            for t in range(cur["ntile"]):
                dma("sp", cur["y_d"][:, :, t * 512:(t + 1) * 512], X[:, :, t * 512:(t + 1) * 512], [sX[t * 8 + c] for c in range(8)], [new_out()], f"x{t}")
            P.op("sp", lambda e: e.nop(), list(out_slots), [])

    def body_inner():
        stg["n"] = 0
        psc["n"] = 0
        P.phase = "prologue"
        del out_slots[:]
        if not P.dry:
            WB.convert(lambda cid: cid[0] in ("gu", "d") and cid[1] == 0 and cid[2] == 0)
        memset(ONESB[:, :], 1.0 / 1024.0, [sONES])
        memset(BD64[:, :], 0.0, [sBD64])
        memset(BD64[0:64, 0:64], 1.0 / 64.0, [sBD64])
        memset(BD64[64:128, 64:128], 1.0 / 64.0, [sBD64])
        memset(ONES64[:, :], 1.0, [sONES64])
        dma("sp", CC[:, :], cc_d, [], [sCC], "c0")
        dma("sp", BMOD[:, :, :].rearrange("p a b -> p (a b)"), bmod_d, [], [sBMOD], "c1")
        dma("sp", GV[:, :, :, :].rearrange("p a b c -> p (a b c)"), gv_d, [], [sGV], "c2")
        dma("sp", GQK[:, :, :].rearrange("p a b -> p (a b)"), gqk_d, [], [sGQK], "c3")
        dma("sp", GKB[:, :, :].rearrange("p a b -> p (a b)"), gkb_d, [], [sGKB], "c4")
        dma("sp", PCV[:, :, :, :].rearrange("p a b c -> p (a b c)"), pcv_d, [], [sPCV], "c5")
        dma("pool", WPB[:, :, :, :].rearrange("p a b c -> p (a b c)"), wpbd_d, [], [sWPB], "c6")
        act(SCB[:, :, :].rearrange("p a b -> p (a b)"), CC[:, :], AF.Silu, [sCC], [sSCB])

        for l in range(2):
            pm = PB[7]
            for piece in range(18):
                buf, bs = WB.get([(lambda b: wview(b, 8, 512), wsrc(wmod_d[l, :, piece * 512:(piece + 1) * 512]))])
                bv = wview(buf, 8, 512)
                for fc in range(4):
                    f = piece * 4 + fc
                    for kc in range(8):
                        mm(pm[:, 2 * f:2 * f + 2], bv[:, kc, fc * 128:(fc + 1) * 128], SCB[:, kc, :],
                           kc == 0, kc == 7, [bs, sSCB], [sPB[7]])
            pmv = pm[:, 0:144].rearrange("p (f s) -> p f s", s=2)
            for s in range(2):
                tt(MODS[:, l, s, :], pmv[:, :, s], BMOD[:, l, :], ALU.add, [sPB[7], sBMOD], [sMODS])
            for s in range(2):
                def mv(i):
                    return MODS[:, l, s, i * 8:(i + 1) * 8]
                for (di, gi, sc_i) in ((0, 0, 1), (3, 1, 4), (6, 2, 7)):
                    stt(DER[:, l, s, di, :], mv(sc_i), 1.0, GV[:, l, gi, :], ALU.add, ALU.mult, [sMODS, sGV], [sDER])
                for (di, sh_i) in ((1, 0), (4, 3), (7, 6)):
                    copy(DER[:, l, s, di, :], mv(sh_i), [sMODS], [sDER])
                ts(DER[:, l, s, 2, :], mv(2), 0.5, None, ALU.mult, None, [sMODS], [sDER])
                copy(DER[:, l, s, 5, :], mv(5), [sMODS], [sDER])
                ts(DER[:, l, s, 8, :], mv(8), 0.5, None, ALU.mult, None, [sMODS], [sDER])

        if not P.dry:
            for lyr in range(2):
                WB.convert(lambda cid: cid[0] in ("gu", "d") and cid[1] == 0 and cid[2] == lyr)
                WB.convert(lambda cid: cid[0] in ("k", "v", "pc", "q") and cid[1] == lyr)
                WB.convert(lambda cid: cid[0] in ("gc", "gb") and cid[1] == lyr)
                WB.convert(lambda cid: cid[0] in ("mg", "o") and cid[1] == lyr)
                WB.convert(lambda cid: cid[0] in ("gu", "d") and cid[1] == 1 and cid[2] == lyr)
            BT.convert(lambda cid: True)

        def der(l, s, i, c):
            return DER[:, l, s, i, c:c + 1]

        def xs_(t, c):
            return sX[t * 8 + c]

        norm_done = {"key": None}
        NTb = [(NT, sNTc), (NT2, sNT2c)]

        def norm_mod(l, s, t, ia):
            if norm_done["key"] == (l, s, t, ia):
                norm_done["key"] = None
                return
            for c in range(8):
                norm_sq((l, s, t, ia), c, sq8=True)
            for c in range(8):
                norm_sqmm((l, s, t, ia), c, sq8=True)
            norm_fin()
            for c in range(8):
                norm_chunk((l, s, t, ia), c)

        def sqbuf(c):
            k = c % 3
            if k == 2:
                return SQ2[:, :], sSQ2
            return SCR[k][:, 0:256].bitcast(BF16), sSCR[k]

        def _sq_eng(s):
            if s == 1:
                return ["act", "dve", "pool", "act", "dve", "act", "dve", "pool"]
            return ["act", "dve", "act", "dve", "act", "dve", "act", "dve"]

        def norm_sq(key, c, sq8=False):
            l, s, t, ia = key
            pp, P.phase = P.phase, "norm"
            xin = X[:, c, t * 512:(t + 1) * 512]
            if sq8:
                sqv, ssl = SQ8[:, c, :], sSQ8c[c]
            else:
                sqv, ssl = sqbuf(c)
            e = _sq_eng(s)[c]
            if e == "act":
                act(sqv, xin, AF.Square, [xs_(t, c)], [ssl])
            else:
                tt(sqv, xin, xin, ALU.mult, [xs_(t, c)], [ssl], eng=e)
            P.phase = pp

        def norm_sqmm(key, c, sq8=False):
            pp, P.phase = P.phase, "norm"
            if sq8:
                sqv, ssl = SQ8[:, c, :], sSQ8c[c]
            else:
                sqv, ssl = sqbuf(c)
            mm(PB[0][:, :], ONESB[:, :], sqv, c == 0, c == 7, [ssl, sONES], [sPB[0]])
            P.phase = pp

        def norm_fin():
            pp, P.phase = P.phase, "norm"
            act(RSTD[:, :], PB[0][:, :], AF.Ln, [sPB[0]], [sRSTD], bias=EPS, scale=1.0)
            act(RSTD[:, :], RSTD[:, :], AF.Exp, [sRSTD], [sRSTD], scale=-0.5)
            P.phase = pp

        def norm_chunk(key, c, nb=0):
            l, s, t, ia = key
            pp, P.phase = P.phase, "norm"
            NTx, sNTx = NTb[nb]
            c0 = t * 512
            k = c % 2
            stt(SCR[k][:, 0:512], X[:, c, c0:c0 + 512], der(l, s, ia, c), RSTD[:, :], ALU.mult, ALU.mult,
                [xs_(t, c), sDER, sRSTD], [sSCR[k]])
            act(NTx[:, c, :], SCR[k][:, 0:512], AF.Identity, [sSCR[k], sDER], [sNTx[c]], bias=der(l, s, ia + 1, c), scale=1.0)
            P.phase = pp

        def ffn(l, s, t, which, nxt=None):
            ia = 0 if which == 0 else 6
            c0 = t * 512
            norm_mod(l, s, t, ia)
            P.phase = "ffn.gu"
            for j2 in range(11):
                P.phase = f"ffn.gu.{j2}"
                buf, bs = WB.get([
                    (lambda b: wview(b, 8, 512)[:, :, 0:256], wsrc(wg_d[which][l, :, j2 * 256:(j2 + 1) * 256])),
                    (lambda b: wview(b, 8, 512)[:, :, 256:512], wsrc(wu_d[which][l, :, j2 * 256:(j2 + 1) * 256])),
                ], cid=("gu", which, l, j2), ncols=4096)
                bv = wview(buf, 8, 512)
                for jj in range(2):
                    j = j2 * 2 + jj
                    ig = 1 + (j % 2)
                    iu = 3 + (j % 2)
                    for kc in range(8):
                        mm(PB[ig][:, :], bv[:, kc, jj * 128:(jj + 1) * 128], NT[:, kc, :], kc == 0, kc == 7, [bs, sNTc[kc]], [sPB[ig]])
                    for kc in range(8):
                        mm(PB[iu][:, :], bv[:, kc, 256 + jj * 128:256 + (jj + 1) * 128], NT[:, kc, :], kc == 0, kc == 7, [bs, sNTc[kc]], [sPB[iu]])
                    k = j % 2
                    act(SCR[k][:, 0:512], PB[ig][:, :], AF.Silu, [sPB[ig]], [sSCR[k]])
                    tt(ACTT[:, j, :], SCR[k][:, 0:512], PB[iu][:, :], ALU.mult, [sSCR[k], sPB[iu]], [sACTT])
            for dc in range(8):
                P.phase = f"ffn.down.{dc}"
                if nxt is not None and dc < 3:
                    for c in ((0, 1, 2), (3, 4, 5), (6, 7))[dc]:
                        norm_sq(nxt, c)
                buf, bs = WB.get([(lambda b: wview(b, 22, 128), wsrc(wd_d[which][l, :, dc * 128:(dc + 1) * 128]))], cid=("d", which, l, dc), ncols=2816)
                bv = wview(buf, 22, 128)
                ip = 5 + (dc % 2)
                for kc in range(22):
                    mm(PB[ip][:, :], bv[:, kc, :], ACTT[:, kc, :], kc == 0, kc == 21, [bs, sACTT], [sPB[ip]])
                stt(X[:, dc, c0:c0 + 512], PB[ip][:, :], der(l, s, ia + 2, dc), X[:, dc, c0:c0 + 512], ALU.mult, ALU.add,
                    [sPB[ip], sDER, xs_(t, dc)], [xs_(t, dc)])
                if nxt is not None:
                    if dc < 3:
                        for c in ((0, 1, 2), (3, 4, 5), (6, 7))[dc]:
                            norm_sqmm(nxt, c)
                    if dc == 2:
                        norm_fin()
                    if 3 <= dc <= 6:
                        norm_chunk(nxt, 2 * (dc - 3))
                        norm_chunk(nxt, 2 * (dc - 3) + 1)
            if nxt is not None:
                norm_done["key"] = nxt

        qkc = {"n": 0}

        def qknorm(ps, sps, dst, sdst, gcol):
            k = qkc["n"] % 2
            qkc["n"] += 1
            if k == 0:
                sqv, ssq = SQ2[:, :], sSQ2
                pb, spb = PB[0], sPB[0]
                rs, srs = RSTD[:, :], sRSTD
            else:
                sqv, ssq = SCR[1][:, 0:256].bitcast(BF16), sSCR[1]
                pb, spb = PB[7], sPB[7]
                rs, srs = SCR[0][:, 0:512], sSCR[0]
            act(sqv, ps, AF.Square, [sps], [ssq])
            mm(pb[:, :], BD64[:, :], sqv, True, True, [ssq, sBD64], [spb])
            act(rs, pb[:, :], AF.Ln, [spb], [srs], bias=EPS, scale=1.0)
            act(rs, rs, AF.Exp, [srs], [srs], scale=-0.5)
            stt(dst, ps, gcol, rs, ALU.mult, ALU.mult, [sps, sGQK, srs], [sdst])

        def win(l, c0, n):
            return wsrc(win_d[l, :, c0:c0 + n])

        def upv(buf, grp, t, c, lo, hi):
            if grp == "p":
                v = buf[:, c, 0:4 * 272].rearrange("p (s m) -> p s m", m=272)
                return v[:, 2 * t:2 * t + 2, 8 + lo:8 + hi]
            return buf[:, c, 8 + t * 512 + lo:8 + t * 512 + hi].unsqueeze(1)

        def seg(ap2, grp, n=None):
            if grp == "p":
                return ap2.rearrange("p (s m) -> p s m", m=256)
            return ap2.unsqueeze(1)

        def mix1(l, s, grp, t):
            c0 = t * 512
            norm_mod(l, s, t, 3)
            P.phase = "mix1"
            NT, sNTc = NTb[t % 2]
            ntile_g = 2 if grp == "p" else 5
            nkey = (l, s, t + 1, 3) if t + 1 < ntile_g else None
            if nkey is not None:
                for c in range(8):
                    norm_sq(nkey, c, sq8=True)
            buf, bs = WB.get([(lambda b: wview(b, 8, 512), win(l, 512, 512))], cid=("k", l), ncols=4096)
            bv = wview(buf, 8, 512)
            for pr in range(4):
                ip = 1 + pr
                for kc in range(8):
                    mm(PB[ip][:, :], bv[:, kc, pr * 128:(pr + 1) * 128], NT[:, kc, :], kc == 0, kc == 7, [bs, sNTc[kc]], [sPB[ip]])
            for pr in range(4):
                ip = 1 + pr
                qknorm(PB[ip][:, :], sPB[ip], KT[:, pr, c0:c0 + 512], sKT, GQK[:, l, 1:2])
            if nkey is not None:
                for c in range(8):
                    norm_sqmm(nkey, c, sq8=True)
                norm_fin()
                for c in range(8):
                    norm_chunk(nkey, c, nb=(t + 1) % 2)
                norm_done["key"] = nkey
            if grp == "p":
                P.phase = "mix1.ktm"
                for tb in range(4):
                    ip = 5 + (tb % 2)
                    for kc in range(8):
                        mm(PB[ip][:, :], NT[:, kc, tb * 128:(tb + 1) * 128], bv[:, kc, :], kc == 0, kc == 7, [bs, sNTc[kc]], [sPB[ip]])
                    act(SCR[0][:, 0:512], PB[ip][:, :], AF.Square, [sPB[ip]], [sSCR[0]])
                    P.op("dve", lambda e: e.tensor_reduce(out=SS8[:, :], in_=SCR[0][:, 0:512].rearrange("p (h d) -> p h d", d=64),
                                                         axis=AX.X, op=ALU.add), [sSCR[0]], [sSS8])
                    act(SS8[:, :], SS8[:, :], AF.Ln, [sSS8], [sSS8], bias=EPS, scale=1.0 / 64.0)
                    act(SS8[:, :], SS8[:, :], AF.Exp, [sSS8], [sSS8], scale=-0.5)
                    okb, sokb, kkey = ((OUTK[:, :], sOUTK, "ok") if tb % 2 == 0 else (RSTD[:, :], sRSTD, "ok2"))
                    tt(okb.rearrange("p (h d) -> p h d", d=64), PB[ip][:, :].rearrange("p (h d) -> p h d", d=64),
                       SS8[:, :].unsqueeze(2).broadcast_to([128, 8, 64]), ALU.mult, [sPB[ip], sSS8], [sokb])
                    tt(okb.rearrange("p (h d) -> p h d", d=64), okb.rearrange("p (h d) -> p h d", d=64),
                       GKB[:, l, :].unsqueeze(1).broadcast_to([128, 8, 64]), ALU.mult, [sokb, sGKB], [sokb])
                    r0 = t * 512 + tb * 128
                    dma("sp", nk_d[r0:r0 + 128, l, :], okb, [sokb], [new_out()], kkey)
            P.phase = "mix1.v"
            buf, bs = WB.get([(lambda b: wview(b, 8, 512), win(l, 1024, 512))], cid=("v", l), ncols=4096)
            bv = wview(buf, 8, 512)
            for tb in range(4):
                ip = 5 + (tb % 2)
                for kc in range(8):
                    mm(PB[ip][:, :], NT[:, kc, tb * 128:(tb + 1) * 128], bv[:, kc, :], kc == 0, kc == 7, [bs, sNTc[kc]], [sPB[ip]])
                copy(VA[:, t * 4 + tb, :], PB[ip][:, :], [sPB[ip]], [sVA], eng="act")
                if grp == "p":
                    ov, sov, okey = ((OUTV[:, :], sOUTV, "ov") if tb % 2 == 0 else (SCR[1][:, 0:512], sSCR[1], "ov2"))
                    copy(ov, PB[ip][:, :], [sPB[ip]], [sov])
                    r0 = t * 512 + tb * 128
                    dma("sp", nv_d[r0:r0 + 128, l, :], ov, [sov], [new_out()], okey)
            P.phase = "mix1.pc"
            buf, bs = WB.get([(lambda b: wview(b, 8, 512), win(l, 1536, 512))], cid=("pc", l), ncols=4096)
            bv = wview(buf, 8, 512)
            for c4 in range(4):
                ip = 1 + (c4 % 2)
                for kc in range(8):
                    mm(PB[ip][:, :], bv[:, kc, c4 * 128:(c4 + 1) * 128], NT[:, kc, :], kc == 0, kc == 7, [bs, sNTc[kc]], [sPB[ip]])
                if c4 < 2:
                    copy(upv(UP, grp, t, c4, 0, 512 if grp == "s" else 256), seg(PB[ip][:, :], grp), [sPB[ip]], [sUP], eng="act")
                else:
                    copy(UCV[:, c4 - 2, :], PB[ip][:, :], [sPB[ip]], [sUCV], eng="act")
            P.phase = "mix1.gc"
            buf, bs = WS.get([(lambda b: wview(b, 8, 256), win(l, 2304, 256))], cid=("gc", l), ncols=2048)
            bv = wview(buf, 8, 256)
            for c2 in range(2):
                ip = 3 + (c2 % 2)
                for kc in range(8):
                    mm(PB[ip][:, :], bv[:, kc, c2 * 128:(c2 + 1) * 128], NT[:, kc, :], kc == 0, kc == 7, [bs, sNTc[kc]], [sPB[ip]])
                tt(upv(ZZ, grp, t, c2, 0, 512 if grp == "s" else 256), seg(UCV[:, c2, :], grp), seg(PB[ip][:, :], grp), ALU.mult,
                   [sUCV, sPB[ip]], [sZZ])

        def attn_prompt(l, t):
            c0 = t * 512
            n = 0
            for sq in range(2):
                for h in range(8):
                    pr, po = h // 2, (h % 2) * 64
                    isb = 1 + (n % 3)
                    k = n % 2
                    io, idn = 4 + k, 6 + k
                    n += 1
                    qap = QT[po:po + 64, pr, sq * 256:(sq + 1) * 256]
                    for kb in range(2):
                        kc0 = c0 + sq * 256 + kb * 128
                        mm(PB[isb][:, kb * 256:(kb + 1) * 256], KT[po:po + 64, pr, kc0:kc0 + 128], qap, True, True, [sKT, sQT], [sPB[isb]])
                    act(PT[k][:, :], PB[isb][:, :], AF.Exp, [sPB[isb]], [sPT[k]], scale=0.125)
                    for kb in range(2):
                        mm(PB[io][0:64, 0:256], VA[:, t * 4 + sq * 2 + kb, h * 64:(h + 1) * 64], PT[k][:, kb * 256:(kb + 1) * 256],
                           kb == 0, kb == 1, [sVA, sPT[k]], [sPB[io]])
                    for kb in range(2):
                        mm(PB[idn][0:64, 0:256], ONES64[:, :], PT[k][:, kb * 256:(kb + 1) * 256],
                           kb == 0, kb == 1, [sONES64, sPT[k]], [sPB[idn]])
                    act(RSTD[0:64, 0:256], PB[idn][0:64, 0:256], AF.Ln, [sPB[idn]], [sRSTD])
                    act(RSTD[0:64, 0:256], RSTD[0:64, 0:256], AF.Exp, [sRSTD], [sRSTD], scale=-1.0)
                    tt(AT[po:po + 64, pr, sq * 256:(sq + 1) * 256], PB[io][0:64, 0:256], RSTD[0:64, 0:256], ALU.mult,
                       [sPB[io], sRSTD], [sAT])

        def attn_sample(l, t):
            cnt = {"s": 0, "e": 0}
            pending = []
            for h in range(8):
                pr, po = h // 2, (h % 2) * 64
                sts = []
                prev = None
                for jb in range(2):
                    j = 2 * t + jb
                    typ = 0 if j == 0 else (2 if j == 9 else 1)
                    brow = min(max(4 * j - 4, 0), 28)
                    if prev is None or prev[0] != typ:
                        bt, bts = BT.get([(lambda b: b[:, :], btab_d[l, typ, h, :, :])], barrier=(jb == 0 and h == 0), cid=("bt", l, typ, h), ncols=1536)
                        prev = (typ, bt, bts)
                    sts.append({"jb": jb, "kch0": brow // 2, "bt": prev[1], "bts": prev[2], "io": 4 + jb, "idn": 6 + jb,
                                "qap": QT[po:po + 64, pr, jb * 256:(jb + 1) * 256], "isb": {}})

                def emit_s(st, cp):
                    isb = 1 + (cnt["s"] % 3)
                    cnt["s"] += 1
                    st["isb"][cp] = isb
                    for ii in range(2):
                        if cp < 3:
                            kc0 = (st["kch0"] + cp * 2 + ii) * 128
                            lhs = KT[po:po + 64, pr, kc0:kc0 + 128]
                            rd = [sKT, sQT]
                        else:
                            i = (cp - 3) * 2 + ii
                            lhs = CKT[po:po + 64, pr, i * 128:(i + 1) * 128]
                            rd = [sCKT, sQT]
                        mm(PB[isb][:, ii * 256:(ii + 1) * 256], lhs, st["qap"], True, True, rd, [sPB[isb]])

                def emit_pv(st, cp):
                    isb = st["isb"][cp]
                    k = cnt["e"] % 2
                    cnt["e"] += 1
                    io, idn = st["io"], st["idn"]
                    if cp < 3:
                        stt(SCR[k][:, 0:512], PB[isb][:, :], 0.125, st["bt"][:, cp * 512:(cp + 1) * 512], ALU.mult, ALU.add,
                            [sPB[isb], st["bts"]], [sSCR[k]])
                        act(PT[k][:, :], SCR[k][:, 0:512], AF.Exp, [sSCR[k]], [sPT[k]])
                    else:
                        act(PT[k][:, :], PB[isb][:, :], AF.Exp, [sPB[isb]], [sPT[k]], scale=0.125)
                    for ii in range(2):
                        if cp < 3:
                            lhs = VA[:, st["kch0"] + cp * 2 + ii, h * 64:(h + 1) * 64]
                            rd = [sVA, sPT[k]]
                        else:
                            i = (cp - 3) * 2 + ii
                            lhs = CVA[:, i, h * 64:(h + 1) * 64]
                            rd = [sCVA, sPT[k]]
                        first = (cp == 0 and ii == 0)
                        last = (cp == 4 and ii == 1)
                        mm(PB[io][0:64, 0:256], lhs, PT[k][:, ii * 256:(ii + 1) * 256], first, last, rd, [sPB[io]])
                        mm(PB[idn][0:64, 0:256], ONES64[:, :], PT[k][:, ii * 256:(ii + 1) * 256], first, last,
                           [sONES64, sPT[k]], [sPB[idn]])

                A, B = sts
                emit_s(A, 0)
                emit_s(B, 0)
                for cp in range(5):
                    if cp + 1 < 5:
                        emit_s(A, cp + 1)
                    emit_pv(A, cp)
                    if cp + 1 < 5:
                        emit_s(B, cp + 1)
                    emit_pv(B, cp)

                def mk_norm(st, po=po, pr=pr):
                    def f():
                        jb = st["jb"]
                        io, idn = st["io"], st["idn"]
                        act(SS64[jb][0:64, 0:256], PB[idn][0:64, 0:256], AF.Ln, [sPB[idn]], [sSS64[jb]])
                        act(SS64[jb][0:64, 0:256], SS64[jb][0:64, 0:256], AF.Exp, [sSS64[jb]], [sSS64[jb]], scale=-1.0)
                        tt(AT[po:po + 64, pr, jb * 256:(jb + 1) * 256], PB[io][0:64, 0:256], SS64[jb][0:64, 0:256], ALU.mult,
                           [sPB[io], sSS64[jb]], [sAT])
                    return f
                mk_norm(A)()
                mk_norm(B)()

        def pool_conv(l, grp, t):
            S, n = (2, 256) if grp == "p" else (1, 512)
            m = n + 16
            if grp == "p":
                ktab = 0
            else:
                ktab = 1 if t == 0 else (3 if t == 4 else 2)
            dma("sp", INVC[:, :, :].rearrange("p a b -> p (a b)"), invc_d[ktab], [], [sINVC], "iv")
            buf, bs = WS.get([(lambda b: wview(b, 8, 256), win(l, 2048, 256))], cid=("gb", l), ncols=2048)
            bv = wview(buf, 8, 256)
            for c in range(2):
                ip = 3 + (c % 2)
                for kc in range(8):
                    mm(PB[ip][:, :], bv[:, kc, c * 128:(c + 1) * 128], NT[:, kc, :], kc == 0, kc == 7, [bs, sNTc[kc]], [sPB[ip]])

            def U(c, lo, hi, p0=0, p1=128):
                return upv(UP, grp, t, c, lo, hi)[p0:p1]

            def tv(h2, lo, hi, p0=0, p1=128):
                return h2.rearrange("p (s m) -> p s m", m=m)[p0:p1, :, 8 + lo:8 + hi]

            def TAv(c, lo, hi, p0=0, p1=128):
                return tv(TA[:, c, 0:S * m], lo, hi, p0, p1)

            def TBv(lo, hi, p0=0, p1=128):
                return tv(SCR[0][:, 0:S * m], lo, hi, p0, p1)

            def TCv(lo, hi, p0=0, p1=128):
                return tv(SCR[1][:, 0:S * m], lo, hi, p0, p1)

            def SUMv(c, p0=0, p1=128):
                return SUMT[p0:p1, c, :].rearrange("p (s m) -> p s m", m=n)

            add = ALU.add
            tt(TAv(0, -7, n + 7, 64, 128), U(0, -8, n + 6, 64, 128), U(0, -7, n + 7, 64, 128), add, [sUP], [sTA])
            tt(TAv(1, -7, n + 7), U(1, -8, n + 6), U(1, -7, n + 7), add, [sUP], [sTA])
            tt(SUMv(0, 0, 64), U(0, -1, n - 1, 0, 64), U(0, 0, n, 0, 64), add, [sUP], [sSUMT])
            tt(SUMv(0, 64, 128), TAv(0, -1, n - 1, 64, 128), TAv(0, 1, n + 1, 64, 128), add, [sTA], [sSUMT])
            tt(TBv(-6, n + 6), TAv(1, -7, n + 5), TAv(1, -5, n + 7), add, [sTA], [sSCR[0]])
            tt(SUMv(1, 0, 64), TBv(-2, n - 2, 0, 64), TBv(2, n + 2, 0, 64), add, [sSCR[0]], [sSUMT])
            tt(TCv(-4, n + 4, 64, 128), TBv(-6, n + 2, 64, 128), TBv(-2, n + 6, 64, 128), add, [sSCR[0]], [sSCR[1]])
            tt(SUMv(1, 64, 128), TCv(-4, n - 4, 64, 128), TCv(4, n + 4, 64, 128), add, [sSCR[1]], [sSUMT])
            for c in range(2):
                tt(SUMT[:, c, :], SUMT[:, c, :], INVC[:, c, :], ALU.mult, [sSUMT, sINVC], [sSUMT])
                tt(seg(DD[:, c, :], grp), SUMv(c), U(c, 0, n), ALU.subtract, [sSUMT, sUP], [sDD])
            for c in range(2):
                ip = 1 + (c % 2)
                mm(PB[ip][:, :], WPB[:, l, c, :], DD[:, c, :], True, True, [sWPB, sDD], [sPB[ip]])
                act(PL[:, c, :], PB[ip][:, :], AF.Identity, [sPB[ip], sPCV], [sPL], scale=PCV[:, l, 0, c:c + 1])

            def Zv(c, lo, hi):
                return upv(ZZ, grp, t, c, lo, hi)

            for c in range(2):
                acc = SUMv(c)
                ts(acc, Zv(c, -1, n - 1), PCV[:, l, 1, c:c + 1], PCV[:, l, 4, c:c + 1], ALU.mult, ALU.add, [sZZ, sPCV, sSUMT], [sSUMT])
                stt(acc, Zv(c, 0, n), PCV[:, l, 2, c:c + 1], acc, ALU.mult, ALU.add, [sZZ, sPCV, sSUMT], [sSUMT])
                stt(acc, Zv(c, 1, n + 1), PCV[:, l, 3, c:c + 1], acc, ALU.mult, ALU.add, [sZZ, sPCV, sSUMT], [sSUMT])
                ip = 3 + (c % 2)
                tt(CVT[:, c, :], SUMT[:, c, :], PB[ip][:, :], ALU.mult, [sSUMT, sPB[ip]], [sCVT])

        def mix2(l, s, grp, t):
            c0 = t * 512
            norm_mod(l, s, t, 3)
            if grp == "s":
                dma("pool", CKT[:, :, :], ckT_d[l].rearrange("a p k -> p a k"), [], [sCKT], "ck")
                dma("pool", CVA[:, :, :], cvv_d[l].rearrange("a p k -> p a k"), [], [sCVA], "cv")
            P.phase = "mix2.q"
            buf, bs = WB.get([(lambda b: wview(b, 8, 512), win(l, 0, 512))], cid=("q", l), ncols=4096)
            bv = wview(buf, 8, 512)
            for pr in range(4):
                ip = 1 + pr
                for kc in range(8):
                    mm(PB[ip][:, :], bv[:, kc, pr * 128:(pr + 1) * 128], NT[:, kc, :], kc == 0, kc == 7, [bs, sNTc[kc]], [sPB[ip]])
            for pr in range(4):
                ip = 1 + pr
                qknorm(PB[ip][:, :], sPB[ip], QT[:, pr, :], sQT, GQK[:, l, 0:1])
            P.phase = "attn." + grp
            if grp == "p":
                attn_prompt(l, t)
            else:
                attn_sample(l, t)
            P.phase = "poolconv"
            pool_conv(l, grp, t)
            P.phase = "merge"
            if DEBUG_LIMIT is not None and grp == "p" and t == 0 and l == 0:
                dma("pool", dbg_d[:, 0:4, :], AT[:, :, :], [sAT], [new_out()], "dbg")
                dma("pool", dbg_d[:, 4:6, :], PL[:, :, :], [sPL], [new_out()], "dbg")
                dma("pool", dbg_d[:, 6:8, :], CVT[:, :, :], [sCVT], [new_out()], "dbg")
            for br, (gc0, wbr, kcn, SRC, ssrc) in enumerate(((2560, wba_d, 4, AT, sAT), (3584, wbp_d, 2, PL, sPL), (4608, wbc_d, 2, CVT, sCVT))):
                for qt in range(4):
                    P.phase = f"merge.{br}.{qt}"
                    nb_cols = kcn * 256
                    gbuf, gs = WB.get([
                        (lambda b: b[:, 0:2048].rearrange("p (k c) -> p k c", c=256), win(l, gc0 + qt * 256, 256)),
                        (lambda b, kcn=kcn: b[:, 2048:2048 + kcn * 256].rearrange("p (k c) -> p k c", c=256),
                         wsrc(wbr[l, :, qt * 256:(qt + 1) * 256])),
                    ], cid=("mg", l, br, qt), ncols=2048 + nb_cols)
                    gv = gbuf[:, 0:2048].rearrange("p (k c) -> p k c", c=256)
                    bbv = gbuf[:, 2048:2048 + nb_cols].rearrange("p (k c) -> p k c", c=256)
                    bbs = gs
                    for d2 in range(2):
                        dc = qt * 2 + d2
                        ig, ib = 1 + (dc % 2), 3 + (dc % 2)
                        for kc in range(8):
                            mm(PB[ig][:, :], gv[:, kc, d2 * 128:(d2 + 1) * 128], NT[:, kc, :], kc == 0, kc == 7, [gs, sNTc[kc]], [sPB[ig]])
                        for kc in range(kcn):
                            mm(PB[ib][:, :], bbv[:, kc, d2 * 128:(d2 + 1) * 128], SRC[:, kc, :], kc == 0, kc == kcn - 1, [bbs, ssrc], [sPB[ib]])
                        k = dc % 2
                        act(SCR[k][:, 0:512], PB[ig][:, :], AF.Sigmoid, [sPB[ig]], [sSCR[k]])
                        if br == 0:
                            tt(MG[:, dc, :], SCR[k][:, 0:512], PB[ib][:, :], ALU.mult, [sSCR[k], sPB[ib]], [sMG[dc]])
                        else:
                            tt(SCR[k][:, 0:512], SCR[k][:, 0:512], PB[ib][:, :], ALU.mult, [sSCR[k], sPB[ib]], [sSCR[k]])
                            tt(MG[:, dc, :], MG[:, dc, :], SCR[k][:, 0:512], ALU.add, [sMG[dc], sSCR[k]], [sMG[dc]])
            P.phase = "outproj"
            for dc in range(8):
                copy(NT[:, dc, :], MG[:, dc, :], [sMG[dc]], [sNTc[dc]], eng="act")
            for half in range(2):
                buf, bs = WB.get([(lambda b: wview(b, 8, 512), wsrc(wout_d[l, :, half * 512:(half + 1) * 512]))], cid=("o", l, half), ncols=4096)
                bv = wview(buf, 8, 512)
                for d4 in range(4):
                    dc = half * 4 + d4
                    ip = 5 + (d4 % 2)
                    for kc in range(8):
                        mm(PB[ip][:, :], bv[:, kc, d4 * 128:(d4 + 1) * 128], NT[:, kc, :], kc == 0, kc == 7, [bs, sNTc[kc]], [sPB[ip]])
                    stt(X[:, dc, c0:c0 + 512], PB[ip][:, :], der(l, s, 5, dc), X[:, dc, c0:c0 + 512], ALU.mult, ALU.add,
                        [sPB[ip], sDER, xs_(t, dc)], [xs_(t, dc)])

        for grp, ntile, x_d, y_d, s in (("p", 2, xp_d, yp_d, 0), ("s", 5, xs_d, ys_d, 1)):
            if grp not in DEBUG_GROUPS:
                continue
            cur["ntile"], cur["y_d"] = ntile, y_d
            for t in range(ntile):
                dma("sp", X[:, :, t * 512:(t + 1) * 512], x_d[:, :, t * 512:(t + 1) * 512], [], [sX[t * 8 + c] for c in range(8)], f"x{t}")
            memset(UP[:, :, :], 0.0, [sUP])
            memset(ZZ[:, :, :], 0.0, [sZZ])
            for l in range(2):
                stage()
                for t in range(ntile):
                    ffn(l, s, t, 0, nxt=((l, s, t + 1, 0) if t + 1 < ntile else (l, s, 0, 3)))
                stage()
                for t in range(ntile):
                    mix1(l, s, grp, t)
                stage()
                for t in range(ntile):
                    mix2(l, s, grp, t)
                stage()
                for t in range(ntile):
                    ffn(l, s, t, 1, nxt=((l, s, t + 1, 6) if t + 1 < ntile else ((l + 1, s, 0, 0) if l == 0 else None)))
            for t in range(ntile):
                dma("sp", y_d[:, :, t * 512:(t + 1) * 512], X[:, :, t * 512:(t + 1) * 512], [sX[t * 8 + c] for c in range(8)], [new_out()], f"x{t}")
        P.op("sp", lambda e: e.nop(), list(out_slots), [])

    def new_out():
        sl = Slot("o")
        out_slots.append(sl)
        return sl

    P.dry = True
    body()
    P.dry = False
    for st in (WB, BT):
        st.reset()
    body()
    P.emit()
    _CACHE["pe_tags"] = [r["tag"] for r in P.ops["pe"]]
    return nc


_CACHE = {}
DEBUG_LIMIT = None
DEBUG_GROUPS = "ps"


class StopBody(Exception):
    pass


def _host_tables(rpb):
    key = np.arange(768)
    krel = key // 64
    kcol = key % 64
    q = np.arange(256)
    qr = q // 64
    qc = q % 64
    cs = np.clip(qc - 8, 0, 48)
    out = np.empty((2, 3, 8, 768, 256), np.float32)
    for typ in range(3):
        qrr = {0: qr, 1: 4 + qr, 2: 8 + qr}[typ]
        ws = {0: 0 * qr, 1: qr, 2: 4 + 0 * qr}[typ]
        valid = ((krel[:, None] >= ws[None, :]) & (krel[:, None] < ws[None, :] + 8)
                 & (kcol[:, None] >= cs[None, :]) & (kcol[:, None] < cs[None, :] + 16))
        ro = np.clip(krel[:, None] - qrr[None, :] + 7, 0, 14)
        co = np.clip(kcol[:, None] - qc[None, :] + 15, 0, 30)
        vals = rpb[:, :, ro, co]
        out[:, typ] = np.where(valid[None, None], vals, np.float32(NEG))
    out = out.reshape(2, 3, 8, 6, 128, 256).transpose(0, 1, 2, 4, 3, 5)
    return np.ascontiguousarray(out).reshape(2, 3, 8, 128, 6 * 256)


def _invc_tables():
    tabs = np.empty((4, 128, 2, 512), np.float32)
    n = np.arange(512)
    for k in range(4):
        if k == 0:
            pos, Lq = n % 256, 256
        elif k == 1:
            pos, Lq = n, 2560
        elif k == 2:
            pos, Lq = n + 1024, 2560
        else:
            pos, Lq = n + 2048, 2560
        for c in range(2):
            for hf in range(2):
                w = POOL_WINDOWS[2 * c + hf]
                lo = np.clip(pos - w // 2, 0, Lq)
                hi = np.clip(pos - w // 2 + w, 0, Lq)
                tabs[k, hf * 64:(hf + 1) * 64, c, :] = (np.float32(1.0) / (hi - lo).astype(np.float32))[None, :]
    return tabs.reshape(4, 128, 1024)


def kernel(x_prompt, x_sample, cache_k, cache_v, c, c_ctx, w_mod, b_mod,
           g_ffn1, w_ffn1_gate, w_ffn1_up, w_ffn1_down, g_mix, w_in, g_q, g_k, rpb,
           w_pool, pool_scale, w_conv, b_conv, w_br_attn, w_br_pool, w_br_conv, w_out,
           g_ffn2, w_ffn2_gate, w_ffn2_up, w_ffn2_down):
    f = lambda a: np.ascontiguousarray(np.asarray(a, dtype=np.float32))
    x_prompt, x_sample, cache_k, cache_v = f(x_prompt), f(x_sample), f(cache_k), f(cache_v)
    c, c_ctx = f(c), f(c_ctx)
    if "nc" not in _CACHE:
        _CACHE["nc"] = build_program()
    nc = _CACHE["nc"]

    def fm(v):
        return v.reshape(8, 128).T

    bmod = f(b_mod).reshape(2, 72, 128).transpose(2, 0, 1).reshape(128, 144)
    gv = np.stack([np.stack([fm(f(g)[l]) for g in (g_ffn1, g_mix, g_ffn2)], axis=1) for l in range(2)], axis=1)
    p64 = np.arange(128) % 64
    gqk = np.stack([np.stack([f(g_q)[l][p64], f(g_k)[l][p64]], axis=1) for l in range(2)], axis=1)
    gkb = np.broadcast_to(f(g_k)[None], (128, 2, 64))
    pcv = np.empty((128, 2, 5, 2), np.float32)
    for l in range(2):
        pcv[:, l, 0, :] = f(pool_scale)[l].reshape(2, 128).T
        for k in range(3):
            pcv[:, l, 1 + k, :] = f(w_conv)[l, k].reshape(2, 128).T
        pcv[:, l, 4, :] = f(b_conv)[l].reshape(2, 128).T
    wpbd = np.zeros((128, 2, 2, 128), np.float32)
    wp = f(w_pool)
    for l in range(2):
        for cchunk in range(2):
            wpbd[0:64, l, cchunk, 0:64] = wp[l, 2 * cchunk]
            wpbd[64:128, l, cchunk, 64:128] = wp[l, 2 * cchunk + 1]
    btab = _host_tables(f(rpb))
    invc = _invc_tables()
    shared = {
        "w_mod": f(w_mod), "bmod": f(bmod), "gv": f(gv.reshape(128, 48)),
        "w_g1": f(w_ffn1_gate), "w_u1": f(w_ffn1_up), "w_d1": f(w_ffn1_down),
        "w_g2": f(w_ffn2_gate), "w_u2": f(w_ffn2_up), "w_d2": f(w_ffn2_down),
        "w_in": f(w_in), "gqk": f(gqk.reshape(128, 4)), "gkb": f(gkb.reshape(128, 128)),
        "btab": btab, "wpbd": f(wpbd.reshape(128, 512)), "pcv": f(pcv.reshape(128, 20)),
        "w_ba": f(w_br_attn), "w_bp": f(w_br_pool), "w_bc": f(w_br_conv), "w_out": f(w_out),
        "invc": invc,
    }
    in_maps = []
    for core in range(8):
        b, half = core // 2, core % 2
        r0 = 0 if half == 0 else 24
        xs = x_sample[b, r0 * 64:r0 * 64 + 2560].reshape(2560, 8, 128).transpose(2, 1, 0)
        xp = x_prompt[4 * core:4 * core + 4].reshape(1024, 8, 128).transpose(2, 1, 0)
        cc = np.stack([fm(c_ctx), fm(c[b])], axis=2).reshape(128, 16)
        ckT = cache_k[b].reshape(2, 4, 2, 512, 64).transpose(0, 1, 2, 4, 3).reshape(2, 4, 128, 512)
        cvv = cache_v[b].reshape(2, 8, 4, 128, 64).transpose(0, 2, 3, 1, 4).reshape(2, 4, 128, 512)
        m = dict(shared)
        m.update({"xs": f(xs), "xp": f(xp), "cc": f(cc), "ckT": f(ckT), "cvv": f(cvv)})
        in_maps.append(m)
    res = run_bass_kernel_spmd(nc, in_maps, core_ids=list(range(8)))
    y_prompt = np.empty((32, 256, 1024), np.float32)
    y_sample = np.empty((4, 4096, 1024), np.float32)
    new_k = np.empty((32, 2, 8, 256, 64), np.float32)
    new_v = np.empty((32, 2, 8, 256, 64), np.float32)
    for core in range(8):
        r = res.results[core]
        b, half = core // 2, core % 2
        yp = np.asarray(r["yp"]).transpose(2, 1, 0).reshape(4, 256, 1024)
        y_prompt[4 * core:4 * core + 4] = yp
        ys = np.asarray(r["ys"]).transpose(2, 1, 0).reshape(2560, 1024)
        if half == 0:
            y_sample[b, 0:2048] = ys[0:2048]
        else:
            y_sample[b, 2048:4096] = ys[512:2560]
        nk = np.asarray(r["nk"]).reshape(4, 256, 2, 8, 64).transpose(0, 2, 3, 1, 4)
        nv = np.asarray(r["nv"]).reshape(4, 256, 2, 8, 64).transpose(0, 2, 3, 1, 4)
        new_k[4 * core:4 * core + 4] = nk
        new_v[4 * core:4 * core + 4] = nv
    return (y_prompt, y_sample, new_k, new_v)
```
